# Optimizing a Trainium2 kernel written in Bass

```python
import functools
import jax, jax.numpy as jnp
from jax import lax
import numpy as np

D_MODEL = 1024
BATCH = 32
SEQ = 256
DEPTH = 1
DEC_BATCH = 8
DEC_SEQ = 1024
PAST_LEN = 256

GRID_W = 64
N_HEADS = 8
N_KV_HEADS = 2
HEAD_DIM = 64
GROUP = N_HEADS // N_KV_HEADS
ATTN_W = N_HEADS * HEAD_DIM
KV_W = N_KV_HEADS * HEAD_DIM
WINDOW = 128
ATTN_SCALE = HEAD_DIM ** -0.5
ROPE_BASE = 10000.0
ROPE_PAIRS_AXIS = HEAD_DIM // 4
D_RNN = 512
N_RNN_BLOCKS = 8
RNN_BLOCK = D_RNN // N_RNN_BLOCKS
CONV_W = 4
CONV_LEFT = 2
RG_C = 8.0
D_MIX = ATTN_W + D_RNN
D_IN = ATTN_W + 2 * KV_W + 2 * D_RNN
N_MOD = 6
PEER_HEADS = 8
N_KEYS = 128
N_EXPERTS = N_KEYS * N_KEYS
PEER_DQ = 256
PEER_TOPK = 16
PEER_BLOCK = 128
EPS = 1e-6

kernel_name = "hymba_rglru_swa_peer_diffusion_step"


def rms_norm(x, g):
    xf = x.astype(jnp.float32)
    y = xf * lax.rsqrt(jnp.mean(xf * xf, axis=-1, keepdims=True) + EPS)
    return (y * g.astype(jnp.float32)).astype(x.dtype)


def modulated_norm(x, g, shift, scale):
    return rms_norm(x, g) * (1 + scale) + shift


def adaln_mods(cvec, w_mod, b_mod):
    m = jnp.einsum('...d,de->...e', jax.nn.silu(cvec), w_mod) + b_mod
    return jnp.split(m, N_MOD, axis=-1)


def centred_dwconv(x, conv_w, conv_b):
    S = x.shape[1]
    xp = jnp.pad(x, ((0, 0), (CONV_LEFT, CONV_W - 1 - CONV_LEFT), (0, 0)))
    y = conv_b
    for j in range(CONV_W):
        y = y + xp[:, j:j + S] * conv_w[j]
    return y


def mixer_inputs(h, w_in, conv_w, conv_b):
    B, S = h.shape[:2]
    p = jnp.einsum('bsd,dp->bsp', h, w_in)
    q, k, v, xr, yg = jnp.split(p, [ATTN_W, ATTN_W + KV_W, ATTN_W + 2 * KV_W, ATTN_W + 2 * KV_W + D_RNN], axis=-1)
    q = q.reshape(B, S, N_HEADS, HEAD_DIM)
    k = k.reshape(B, S, N_KV_HEADS, HEAD_DIM)
    v = v.reshape(B, S, N_KV_HEADS, HEAD_DIM)
    xr = centred_dwconv(xr, conv_w, conv_b)
    return q, k, v, xr, yg


def block_diag(x, w):
    xb = x.reshape(x.shape[:-1] + (N_RNN_BLOCKS, RNN_BLOCK))
    return jnp.einsum('bsnc,ncd->bsnd', xb, w).reshape(x.shape)


def _lin_combine(left, right):
    a1, b1 = left
    a2, b2 = right
    return (a1 * a2, a2 * b1 + b2)


def rglru_direction(xr, w_a, b_a, w_i, b_i, lam, h0, reverse):
    xf = xr.astype(jnp.float32)
    r = jax.nn.sigmoid(block_diag(xf, w_a) + b_a)
    gi = jax.nn.sigmoid(block_diag(xf, w_i) + b_i)
    log_a = -RG_C * r * jax.nn.softplus(-lam.astype(jnp.float32))
    a = jnp.exp(log_a)
    b = jnp.sqrt(-jnp.expm1(2.0 * log_a)) * gi * xf
    a_cum, b_cum = lax.associative_scan(_lin_combine, (a, b), reverse=reverse, axis=1)
    return a_cum * h0[:, None, :].astype(jnp.float32) + b_cum


def rglru_mixer(xr, yg, rg_w_a, rg_b_a, rg_w_i, rg_b_i, rg_lambda, h0_f, h0_b):
    hf = rglru_direction(xr, rg_w_a[0], rg_b_a[0], rg_w_i[0], rg_b_i[0], rg_lambda[0], h0_f, False)
    hb = rglru_direction(xr, rg_w_a[1], rg_b_a[1], rg_w_i[1], rg_b_i[1], rg_lambda[1], h0_b, True)
    o = ((hf + hb) * jax.nn.gelu(yg.astype(jnp.float32))).astype(xr.dtype)
    return o, hf[:, -1], hb[:, 0]


def axial_rope(rows):
    row = jnp.repeat(jnp.arange(rows, dtype=jnp.float32), GRID_W)
    col = jnp.tile(jnp.arange(GRID_W, dtype=jnp.float32), rows)
    inv = ROPE_BASE ** (-jnp.arange(ROPE_PAIRS_AXIS, dtype=jnp.float32) / ROPE_PAIRS_AXIS)
    ang = jnp.concatenate([row[:, None] * inv, col[:, None] * inv], axis=-1)
    return jnp.cos(ang), jnp.sin(ang)


def apply_rope(x, cos, sin):
    xf = x.astype(jnp.float32)
    c = cos[None, :, None, :]
    s = sin[None, :, None, :]
    x1, x2 = xf[..., :HEAD_DIM // 2], xf[..., HEAD_DIM // 2:]
    return jnp.concatenate([x1 * c - x2 * s, x2 * c + x1 * s], axis=-1).astype(x.dtype)


def context_attention(q, k, v, sink):
    B, S = q.shape[:2]
    qg = q.reshape(B, S, N_KV_HEADS, GROUP, HEAD_DIM)
    s = jnp.einsum('bqkgd,bckd->bkgqc', qg, k).astype(jnp.float32) * ATTN_SCALE
    sk = jnp.broadcast_to(sink.astype(jnp.float32).reshape(1, N_KV_HEADS, GROUP, 1, 1), s.shape[:-1] + (1,))
    p = jax.nn.softmax(jnp.concatenate([s, sk], axis=-1), axis=-1)[..., :-1]
    o = jnp.einsum('bkgqc,bckd->bqkgd', p.astype(v.dtype), v)
    return o.reshape(B, S, ATTN_W)


def band_blocks(t, nb):
    B = t.shape[0]
    tp = jnp.pad(t, ((0, 0), (WINDOW, WINDOW), (0, 0), (0, 0))).reshape(B, nb + 2, WINDOW, N_KV_HEADS, HEAD_DIM)
    return jnp.concatenate([tp[:, :-2], tp[:, 1:-1], tp[:, 2:]], axis=2)


def latent_attention(q, k, v, sink, *, cos, sin, k_ctx, v_ctx):
    B, S = q.shape[:2]
    nb = S // WINDOW
    q = apply_rope(q, cos, sin)
    k = apply_rope(k, cos, sin)
    qb = q.reshape(B, nb, WINDOW, N_KV_HEADS, GROUP, HEAD_DIM)
    k_band = band_blocks(k, nb)
    v_band = band_blocks(v, nb)
    s_band = jnp.einsum('bnqkgd,bnckd->bkgnqc', qb, k_band).astype(jnp.float32) * ATTN_SCALE
    s_ctx = jnp.einsum('bnqkgd,bckd->bkgnqc', qb, k_ctx).astype(jnp.float32) * ATTN_SCALE
    blk = jnp.arange(nb)[:, None, None]
    qpos = blk * WINDOW + jnp.arange(WINDOW)[None, :, None]
    kpos = (blk - 1) * WINDOW + jnp.arange(3 * WINDOW)[None, None, :]
    mask = (jnp.abs(qpos - kpos) <= WINDOW) & (kpos >= 0) & (kpos < S)
    s_band = jnp.where(mask, s_band, -jnp.inf)
    sk = jnp.broadcast_to(sink.astype(jnp.float32).reshape(1, N_KV_HEADS, GROUP, 1, 1, 1), s_ctx.shape[:-1] + (1,))
    p = jax.nn.softmax(jnp.concatenate([s_band, s_ctx, sk], axis=-1), axis=-1)
    n_band = 3 * WINDOW
    n_ctx = k_ctx.shape[1]
    p_band = p[..., :n_band].astype(v.dtype)
    p_ctx = p[..., n_band:n_band + n_ctx].astype(v.dtype)
    o = (jnp.einsum('bkgnqc,bnckd->bnqkgd', p_band, v_band)
         + jnp.einsum('bkgnqc,bckd->bnqkgd', p_ctx, v_ctx))
    return o.reshape(B, S, ATTN_W)


def peer(h, w_query, sub_keys, u_tab, v_tab):
    B, S, D = h.shape
    xt = h.reshape((B * S) // PEER_BLOCK, PEER_BLOCK, D)

    def retrieve(xb):
        q = jnp.einsum('td,dhq->thq', xb, w_query)
        q1, q2 = jnp.split(q, 2, axis=-1)
        s1 = jnp.einsum('thq,hkq->thk', q1, sub_keys[:, 0]).astype(jnp.float32)
        s2 = jnp.einsum('thq,hkq->thk', q2, sub_keys[:, 1]).astype(jnp.float32)
        v1, i1 = lax.top_k(s1, PEER_TOPK)
        v2, i2 = lax.top_k(s2, PEER_TOPK)
        cand = (v1[..., :, None] + v2[..., None, :]).reshape(PEER_BLOCK, PEER_HEADS, PEER_TOPK * PEER_TOPK)
        cid = (i1[..., :, None] * N_KEYS + i2[..., None, :]).reshape(PEER_BLOCK, PEER_HEADS, PEER_TOPK * PEER_TOPK)
        best, pos = lax.top_k(cand, PEER_TOPK)
        eid = jnp.take_along_axis(cid, pos, axis=-1)
        g = jax.nn.softmax(best, axis=-1)
        u = jnp.take(u_tab, eid, axis=0)
        act = jax.nn.gelu(jnp.einsum('td,thkd->thk', xb, u).astype(jnp.float32))
        vv = jnp.take(v_tab, eid, axis=0)
        return jnp.einsum('thk,thkd->td', (g * act).astype(vv.dtype), vv)

    return lax.map(retrieve, xt).reshape(B, S, D)


def trunk_layer(x, mods, lw, attend, h0_f, h0_b):
    (g_mix, g_ffn, w_in, conv_w, conv_b, rg_w_a, rg_b_a, rg_w_i, rg_b_i, rg_lambda,
     attn_sink, w_out, pq, psk, pu, pv) = lw
    sh1, sc1, ga1, sh2, sc2, ga2 = mods
    h = modulated_norm(x, g_mix, sh1, sc1)
    q, k, v, xr, yg = mixer_inputs(h, w_in, conv_w, conv_b)
    o_att = attend(q, k, v, attn_sink)
    o_rnn, hf, hb = rglru_mixer(xr, yg, rg_w_a, rg_b_a, rg_w_i, rg_b_i, rg_lambda, h0_f, h0_b)
    mix = jnp.concatenate([o_att, o_rnn.astype(o_att.dtype)], axis=-1)
    x = x + ga1 * jnp.einsum('bsm,md->bsd', mix, w_out)
    h2 = modulated_norm(x, g_ffn, sh2, sc2)
    x = x + ga2 * peer(h2, pq, psk, pu, pv)
    return x, k, v, hf, hb


def setup_inputs(seed: int = 0) -> dict:
    key = jax.random.key(seed)
    ks = jax.random.split(key, 32)
    f32 = jnp.float32
    nrm = lambda k, shape, s: jax.random.normal(k, shape, f32) * s
    a0 = jax.random.uniform(ks[13], (DEPTH, 2, D_RNN), f32, 0.9, 0.999)
    sig = a0 ** (1.0 / RG_C)
    rg_lambda = jnp.log(sig) - jnp.log1p(-sig)
    return {
        "x_prompt": nrm(ks[0], (BATCH, SEQ, D_MODEL), 1.0),
        "x_sample": nrm(ks[1], (DEC_BATCH, DEC_SEQ, D_MODEL), 1.0),
        "cache_k": nrm(ks[2], (DEC_BATCH, DEPTH, PAST_LEN, N_KV_HEADS, HEAD_DIM), 1.0),
        "cache_v": nrm(ks[3], (DEC_BATCH, DEPTH, PAST_LEN, N_KV_HEADS, HEAD_DIM), 1.0),
        "state_rnn": nrm(ks[4], (DEC_BATCH, DEPTH, 2, D_RNN), 0.5),
        "c": nrm(ks[5], (DEC_BATCH, D_MODEL), 1.0),
        "c_ctx": nrm(ks[6], (D_MODEL,), 1.0),
        "w_mod": nrm(ks[7], (DEPTH, D_MODEL, N_MOD * D_MODEL), 0.5 * D_MODEL ** -0.5),
        "b_mod": nrm(ks[8], (DEPTH, N_MOD * D_MODEL), 0.02),
        "g_norm_mix": 1.0 + nrm(ks[9], (DEPTH, D_MODEL), 0.1),
        "g_norm_ffn": 1.0 + nrm(ks[10], (DEPTH, D_MODEL), 0.1),
        "w_in": nrm(ks[11], (DEPTH, D_MODEL, D_IN), D_MODEL ** -0.5),
        "conv_w": nrm(ks[12], (DEPTH, CONV_W, D_RNN), CONV_W ** -0.5),
        "conv_b": nrm(ks[14], (DEPTH, D_RNN), 0.02),
        "rg_w_a": nrm(ks[15], (DEPTH, 2, N_RNN_BLOCKS, RNN_BLOCK, RNN_BLOCK), RNN_BLOCK ** -0.5),
        "rg_b_a": nrm(ks[16], (DEPTH, 2, D_RNN), 0.02),
        "rg_w_i": nrm(ks[17], (DEPTH, 2, N_RNN_BLOCKS, RNN_BLOCK, RNN_BLOCK), RNN_BLOCK ** -0.5),
        "rg_b_i": nrm(ks[18], (DEPTH, 2, D_RNN), 0.02),
        "rg_lambda": rg_lambda,
        "attn_sink": nrm(ks[19], (DEPTH, N_HEADS), 0.5),
        "w_out": nrm(ks[20], (DEPTH, D_MIX, D_MODEL), D_MIX ** -0.5),
        "peer_w_query": nrm(ks[21], (DEPTH, D_MODEL, PEER_HEADS, PEER_DQ), D_MODEL ** -0.5),
        "peer_sub_keys": nrm(ks[22], (DEPTH, PEER_HEADS, 2, N_KEYS, PEER_DQ // 2), (PEER_DQ // 2) ** -0.5),
        "peer_u": nrm(ks[23], (DEPTH, N_EXPERTS, D_MODEL), D_MODEL ** -0.5),
        "peer_v": nrm(ks[24], (DEPTH, N_EXPERTS, D_MODEL), PEER_HEADS ** -0.5),
        "g_final": 1.0 + nrm(ks[25], (D_MODEL,), 0.1),
    }


def reference(x_prompt, x_sample, cache_k, cache_v, state_rnn, c, c_ctx, w_mod, b_mod,
              g_norm_mix, g_norm_ffn, w_in, conv_w, conv_b, rg_w_a, rg_b_a, rg_w_i, rg_b_i,
              rg_lambda, attn_sink, w_out, peer_w_query, peer_sub_keys, peer_u, peer_v, g_final):
    n_lat = x_sample.shape[1]
    rows = n_lat // GRID_W
    cos, sin = axial_rope(rows)
    xp, xs = x_prompt, x_sample
    h_zero = jnp.zeros((x_prompt.shape[0], D_RNN), jnp.float32)
    ks_out, vs_out, hs_out = [], [], []
    for l in range(DEPTH):
        lw = (g_norm_mix[l], g_norm_ffn[l], w_in[l], conv_w[l], conv_b[l], rg_w_a[l], rg_b_a[l],
              rg_w_i[l], rg_b_i[l], rg_lambda[l], attn_sink[l], w_out[l], peer_w_query[l],
              peer_sub_keys[l], peer_u[l], peer_v[l])
        mods_ctx = adaln_mods(c_ctx, w_mod[l], b_mod[l])
        mods_lat = [m[:, None, :] for m in adaln_mods(c, w_mod[l], b_mod[l])]
        xp, k_l, v_l, hf_l, hb_l = trunk_layer(xp, mods_ctx, lw, context_attention, h_zero, h_zero)
        ks_out.append(k_l)
        vs_out.append(v_l)
        hs_out.append(jnp.stack([hf_l, hb_l], axis=1))
        lat_attend = functools.partial(latent_attention, cos=cos, sin=sin,
                                       k_ctx=cache_k[:, l], v_ctx=cache_v[:, l])
        xs, _, _, _, _ = trunk_layer(xs, mods_lat, lw, lat_attend, state_rnn[:, l, 0], state_rnn[:, l, 1])
    y_prompt = rms_norm(xp, g_final)
    y_sample = rms_norm(xs, g_final)
    new_k = jnp.stack(ks_out, axis=1)
    new_v = jnp.stack(vs_out, axis=1)
    new_rnn = jnp.stack(hs_out, axis=1)
    return (y_prompt, y_sample, new_k, new_v, new_rnn)
```

```python
import numpy as np
from contextlib import ExitStack
import concourse.bass as bass
import concourse.mybir as mybir
from concourse.bass_utils import run_bass_kernel_spmd

F32 = mybir.dt.float32
BF16 = mybir.dt.bfloat16
I32 = mybir.dt.int32
U32 = mybir.dt.uint32
AF = mybir.ActivationFunctionType
ALU = mybir.AluOpType
AX = mybir.AxisListType

NCORES = 8
D = 1024
KC = 8
NT = 16
NTP = 8
SEQP = 256
SEQS = 1024
D_IN = 1792
ATTN_SCALE = 64 ** -0.5
EPS = 1e-6
NEXP = 16384
GELU_C = 0.7978845608028654

STAGE_B = True
STOP = None
DEBUG = False
_LAST = {}


class StopBuild(Exception):
    pass


_KBREF = {}


def _chk(name):
    if STOP == name:
        _KBREF["kb"].stop = True
NGB = 16
GS = 4


class Buf:
    def __init__(self, name):
        self.name = name
        self.w = {}
        self.r = {}


class TB(Buf):
    def __init__(self, name, t):
        super().__init__(name)
        self.t = t

    def __getitem__(self, k):
        return self.t[k]


class View:
    def __init__(self, base, ap):
        self._b = base
        self.ap = ap
        self.name = base.name

    @property
    def w(self):
        return self._b.w

    @w.setter
    def w(self, v):
        self._b.w = v

    @property
    def r(self):
        return self._b.r

    @r.setter
    def r(self, v):
        self._b.r = v

    def __getitem__(self, k):
        return self.ap[k]


class KB:
    def __init__(self, nc, es):
        self.nc = nc
        self.es = es
        self.es_sem = es
        self.stop = False
        _KBREF["kb"] = self
        self.engs = {"pe": nc.tensor, "act": nc.scalar, "dve": nc.vector, "pool": nc.gpsimd, "sp": nc.sync}
        self.sems = {}
        self.cnt = {}
        self.waited = {e: {} for e in self.engs}
        for e in self.engs:
            self.newsem(e)

    def newsem(self, name):
        self.sems[name] = self.es_sem.enter_context(self.nc.semaphore("s_" + name))
        self.cnt[name] = 0

    def sb(self, name, shape, dt):
        nbytes = int(np.prod(shape[1:])) * (2 if dt == BF16 else 4)
        self.acct = getattr(self, "acct", [])
        self.acct.append((id(self.es), name, nbytes))
        return TB(name, self.es.enter_context(self.nc.sbuf_tensor("sb_" + name, list(shape), dt)))

    def ps(self, name, shape, dt=F32):
        return TB(name, self.es.enter_context(self.nc.psum_tensor("pp_" + name, list(shape), dt)))

    def _wait(self, eng, sname, val):
        if eng == "pe" and sname == "pe":
            return
        if self.waited[eng].get(sname, 0) >= val:
            return
        self.engs[eng].wait_ge(self.sems[sname], val)
        self.waited[eng][sname] = val

    def _deps(self, eng, reads, writes, extra):
        for b in reads:
            for s, v in b.w.items():
                self._wait(eng, s, v)
        for b in writes:
            for s, v in b.w.items():
                self._wait(eng, s, v)
            for s, v in b.r.items():
                self._wait(eng, s, v)
        for (s, v) in extra:
            self._wait(eng, s, v)

    def _update(self, tok, reads, writes, join):
        s, v = tok
        for b in writes:
            if join:
                b.w[s] = max(b.w.get(s, 0), v)
            else:
                b.w = {s: v}
            b.r = {}
        for b in reads:
            if b in writes:
                continue
            b.r[s] = max(b.r.get(s, 0), v)

    def op(self, eng, fn, reads=(), writes=(), extra=()):
        if self.stop:
            return ("sp", 0)
        self._deps(eng, reads, writes, extra)
        inst = fn(self.engs[eng])
        self.cnt[eng] += 1
        inst.then_inc(self.sems[eng], 1)
        tok = (eng, self.cnt[eng])
        self._update(tok, reads, writes, False)
        return tok

    def dma(self, q, out, in_, sem, reads=(), writes=(), extra=(), join=False, **kw):
        if self.stop:
            return ("sp", 0)
        if sem not in self.sems:
            self.newsem(sem)
        if join:
            for b in reads:
                for s, v in b.w.items():
                    self._wait(q, s, v)
            for b in writes:
                for s, v in b.w.items():
                    if s != sem:
                        self._wait(q, s, v)
                for s, v in b.r.items():
                    self._wait(q, s, v)
            for (s, v) in extra:
                self._wait(q, s, v)
        else:
            self._deps(q, reads, writes, extra)
        inst = self.engs[q].dma_start(out=out, in_=in_, **kw)
        self.cnt[sem] += 16
        inst.then_inc(self.sems[sem], 16)
        tok = (sem, self.cnt[sem])
        self._update(tok, reads, writes, join)
        return tok

    def barrier(self, dma_sems=()):
        if self.stop:
            return
        snap = [(e, self.cnt[e]) for e in ("pe", "act", "dve", "pool", "sp")]
        snap += [(s_, self.cnt[s_]) for s_ in dma_sems if s_ in self.cnt]
        for e in self.engs:
            for (s_, v) in snap:
                if v > 0 and s_ != e:
                    self._wait(e, s_, v)

    def gather(self, out, table, idx_ap, sem, reads=(), writes=()):
        if self.stop:
            return ("sp", 0)
        if sem not in self.sems:
            self.newsem(sem)
        self._deps("pool", reads, writes, ())
        inst = self.nc.gpsimd.indirect_dma_start(
            out=out, out_offset=None, in_=table,
            in_offset=bass.IndirectOffsetOnAxis(ap=idx_ap, axis=0))
        self.cnt[sem] += 16
        inst.then_inc(self.sems[sem], 16)
        tok = (sem, self.cnt[sem])
        self._update(tok, reads, writes, False)
        return tok


def build_nc():
    nc = bass.Bass("TRN2", target_bir_lowering=False)

    def din(name, shape, dt=F32):
        return nc.dram_tensor(name, list(shape), dt, kind="ExternalInput").ap()

    def dout(name, shape, dt=F32):
        return nc.dram_tensor(name, list(shape), dt, kind="ExternalOutput").ap()

    xp_d = din("xp", [1024, D])
    xs_d = din("xs", [1024, D])
    ckT_d = din("ckT", [64, 2, 256])
    cv_d = din("cv", [256, 128])
    stT_d = din("stT", [128, 8])
    cT_d = din("cT", [128, 16])
    wmod_d = din("w_mod", [D, 6144])
    bmod_d = din("b_mod", [1, 6144])
    gmix_d = din("g_mix", [1, D])
    gffn_d = din("g_ffn", [1, D])
    gfin_d = din("g_fin", [1, D])
    win_d = din("w_in", [D, D_IN])
    wsw_d = din("w_sw", [D, 640])
    fv_d = din("fvec", [128, 48])
    rgw_d = din("rgw", [128, 16 * 128])
    sink_d = din("sink", [128, 4])
    wout_d = din("w_out", [D, D])
    wq_d = din("w_q", [D, 2048])
    skT_d = din("skT", [128, 16 * 128])
    pu_d = din("peer_u", [NEXP, D])
    pv_d = din("peer_v", [NEXP, D])
    ident_d = din("ident", [128, 128])
    cos_d = din("cosT", [64, 1024])
    sin_d = din("sinT", [64, 1024])
    mask_d = din("mask3", [128, 384])
    iota_d = din("iota16", [128, 16])
    onesv_d = din("onesv", [128, 256])

    yp_d = dout("y_p", [1024, D])
    ys_d = dout("y_s", [1024, D])
    nk_d = dout("nk", [1024, 128])
    nv_d = dout("nv", [1024, 128])
    rn_d = dout("rn", [32, 128])

    uv16 = nc.dram_tensor("uv16_scr", [NEXP, 2048], BF16, kind="Internal").ap()
    mods_d = nc.dram_tensor("mods_scr", [2, 6144], F32, kind="Internal").ap()
    x1_d = nc.dram_tensor("x1_scr", [2048, D], F32, kind="Internal").ap()

    es = ExitStack()
    with es:
        kb = KB(nc, es)
        op, dma = kb.op, kb.dma

        def dbg(name, tb, ap, shape, dt=F32):
            if DEBUG and not kb.stop:
                d = nc.dram_tensor("dbg_" + name, list(shape), dt, kind="ExternalOutput").ap()
                dma("sp", d, ap, "dbg", reads=[tb])

        psT = kb.ps("psT", [128, 1024])
        psP = [kb.ps(f"psP{i}", [128, 512]) for i in range(2)]
        psK = kb.ps("psK", [128, 512])
        psND = Buf("psND")
        psB = [kb.ps(f"psB{i}", [128, 512]) for i in range(2)]
        psC = kb.ps("psC", [128, 512])

        ident = kb.sb("ident", [128, 128], F32)
        cosT = kb.sb("cosT", [64, 1024], F32)
        sinT = kb.sb("sinT", [64, 1024], F32)
        mask3 = kb.sb("mask3", [128, 384], BF16)
        iota16 = kb.sb("iota16", [128, 16], F32)
        onesv = kb.sb("onesv", [128, 256], BF16)
        fvec = kb.sb("fvec", [128, 48], F32)
        sinkT = kb.sb("sinkT", [128, 4], F32)
        esink = kb.sb("esink", [128, 4], F32)
        stT = kb.sb("stT", [128, 8], F32)
        dma("sp", ident[:], ident_d, "c_ident", writes=[ident])
        dma("sp", cosT[:], cos_d, "c_cos", writes=[cosT])
        dma("sp", sinT[:], sin_d, "c_sin", writes=[sinT])
        dma("pool", mask3[:], mask_d, "c_mask", writes=[mask3])
        dma("sp", iota16[:], iota_d, "c_iota", writes=[iota16])
        dma("pool", onesv[:], onesv_d, "c_ones", writes=[onesv])
        dma("sp", fvec[:], fv_d, "c_fvec", writes=[fvec])
        dma("sp", sinkT[:], sink_d, "c_sink", writes=[sinkT])
        dma("sp", stT[:], stT_d, "c_st", writes=[stT])
        op("act", lambda e: e.activation(out=esink[:], in_=sinkT[:], func=AF.Exp), reads=[sinkT], writes=[esink])
        rgs = kb.sb("rgs", [128, 16], F32)
        rgtmp = kb.sb("rgtmp", [128, 8], F32)
        op("act", lambda e: e.activation(out=rgtmp[:], in_=fvec[:, 36:44], func=AF.Exp, scale=-1.0),
           reads=[fvec], writes=[rgtmp])
        op("act", lambda e: e.activation(out=rgtmp[:], in_=rgtmp[:], func=AF.Ln, bias=1.0, scale=1.0),
           reads=[rgtmp], writes=[rgtmp])
        op("dve", lambda e: e.tensor_scalar(out=rgs[:, 0:8], in0=rgtmp[:], scalar1=-8.0, scalar2=None, op0=ALU.mult),
           reads=[rgtmp], writes=[rgs])
        op("dve", lambda e: e.tensor_scalar(out=rgs[:, 8:16], in0=rgtmp[:], scalar1=-16.0, scalar2=None, op0=ALU.mult),
           reads=[rgtmp], writes=[rgs])

        Gb = kb.sb("Gb", [128, D], F32)
        SHb = kb.sb("SHb", [128, D], F32)
        GAb = kb.sb("GAb", [128, D], F32)
        gtmp = kb.sb("gtmp", [128, D], F32)
        wbig = kb.sb("wbig", [128, 19456], BF16)
        w_in = View(wbig, wbig[:, 0:14336].rearrange("p (k n) -> p k n", k=KC))
        wsw = View(wbig, wbig[:, 14336:19456].rearrange("p (k n) -> p k n", k=KC))
        wq = View(wbig, wbig[:, 0:16384].rearrange("p (k n) -> p k n", k=KC))
        win_v = win_d.rearrange("(kc p) n -> p kc n", p=128)
        for kc in range(KC):
            dma("pool", w_in[:, kc, :], win_v[:, kc, :], "w_in", writes=[w_in], join=(kc > 0))
        wsw_v = wsw_d.rearrange("(kc p) n -> p kc n", p=128)
        for kc in range(0, KC, 4):
            dma("pool", wsw[:, kc:kc + 4, :], wsw_v[:, kc:kc + 4, :], "w_in", writes=[wsw], join=True)

        with ExitStack() as es0:
            kb.es = es0
            cT = kb.sb("cT", [128, 16], F32)
            sgT = kb.sb("sgT", [128, 16], F32)
            sT = kb.sb("sT", [128, 16], F32)
            bmod2 = kb.sb("bmod2", [2, 6144], F32)
            wm = [kb.sb(f"wm{i}", [128, KC, 512], F32) for i in range(4)]
            rows = [kb.sb(f"rows{i}", [2, 512], F32) for i in range(2)]
            dma("sp", cT[:], cT_d, "cT", writes=[cT])
            dma("sp", bmod2[:], bmod_d.to_broadcast([2, 6144]), "bmod", writes=[bmod2])
            op("act", lambda e: e.activation(out=sgT[:], in_=cT[:], func=AF.Sigmoid), reads=[cT], writes=[sgT])
            op("dve", lambda e: e.tensor_tensor(out=sT[:], in0=cT[:], in1=sgT[:], op=ALU.mult),
               reads=[cT, sgT], writes=[sT])
            wmod_v = wmod_d.rearrange("(kc p) n -> p kc n", p=128)
            dbg("sT", sT, sT[:], [128, 16])
            dbg("bmod2", bmod2, bmod2[:, 0:512], [2, 512])
            def load_wm(nb):
                w = wm[nb % 4]
                for hf in range(2):
                    dma("sp", w[:, hf * 4:(hf + 1) * 4, :], wmod_v[:, hf * 4:(hf + 1) * 4, nb * 512:(nb + 1) * 512],
                        f"wm{nb % 4}", writes=[w], join=(hf == 1))

            for nb in range(3):
                load_wm(nb)
            for nb in range(12):
                if nb + 3 < 12:
                    load_wm(nb + 3)
                w = wm[nb % 4]
                pm = psP[nb % 2]
                for kc in range(KC):
                    op("pe", lambda e, kc=kc, w=w, pm=pm: e.matmul(
                        pm[0:2, :], lhsT=sT[:, 2 * kc:2 * kc + 2], rhs=w[:, kc, :], start=(kc == 0), stop=(kc == KC - 1)),
                       reads=[sT, w], writes=[pm])
                r = rows[nb % 2]
                op("dve", lambda e, r=r, pm=pm, nb=nb: e.tensor_tensor(
                    out=r[:], in0=pm[0:2, :], in1=bmod2[:, nb * 512:(nb + 1) * 512], op=ALU.add),
                   reads=[pm, bmod2], writes=[r])
                dma("sp", mods_d[:, nb * 512:(nb + 1) * 512], r[:], f"mods_out{nb % 2}", reads=[r])
                if nb == 0:
                    dbg("rows0", r, r[:], [2, 512])
                    dbg("wm0", w, w[:], [128, KC, 512])
            kb.barrier(dma_sems=[k_ for k_ in kb.cnt if k_ not in kb.engs])
            kb.es = es
        mods_done = [("mods_out0", kb.cnt["mods_out0"]), ("mods_out1", kb.cnt["mods_out1"])]
        if DEBUG:
            dm = nc.dram_tensor("dbg_modsd", [2, 6144], F32, kind="ExternalOutput").ap()
            dma("sp", dm, mods_d, "dbg", extra=mods_done)
        stopped = STOP == "mods"

        def load_mods(which, stage, parts=("pro", "ga")):
            base = 0 if stage == "A" else 3 * D
            gsrc = gmix_d if stage == "A" else gffn_d

            def row(off):
                return mods_d[which:which + 1, base + off:base + off + D].to_broadcast([128, D])
            if "ga" in parts:
                dma("sp", GAb[:], row(2 * D), "m_ga", writes=[GAb], extra=mods_done)
            if "pro" in parts:
                dma("sp", SHb[:], row(0), "m_sh", writes=[SHb], extra=mods_done)
                dma("sp", Gb[:], row(D), "m_g", writes=[Gb], extra=mods_done)
                dma("sp", gtmp[:], gsrc.to_broadcast([128, D]), "m_gt", writes=[gtmp])
                op("dve", lambda e: e.scalar_tensor_tensor(out=Gb[:], in0=Gb[:], scalar=1.0, in1=gtmp[:],
                                                           op0=ALU.add, op1=ALU.mult),
                   reads=[gtmp], writes=[Gb])

        uvbuf = Buf("uvbuf")
        conv_i = [0]

        def conv_chunks(n):
            for _ in range(n):
                i = conv_i[0]
                if i >= 32:
                    return
                conv_i[0] += 1
                tab = pu_d if i < 16 else pv_d
                col = 0 if i < 16 else 1024
                r = (i % 16) * 1024
                dma("pool", uv16[r:r + 1024, col:col + 1024], tab[r:r + 1024, :], "conv", writes=[uvbuf], join=True)

        junk_act = kb.sb("junk_act", [128, D], BF16)
        junk_dve = kb.sb("junk_dve", [128, D], BF16)

        def norm_stats(xt, name_i, stats):
            ss, rt, rstd = stats
            op("act", lambda e: e.activation(out=junk_act[:], in_=xt[:], func=AF.Square, accum_out=ss[:]),
               reads=[xt], writes=[ss])
            op("act", lambda e: e.activation(out=rt[:], in_=ss[:], func=AF.Sqrt, scale=1.0 / D, bias=EPS),
               reads=[ss], writes=[rt])
            op("dve", lambda e: e.reciprocal(out=rstd[:], in_=rt[:]), reads=[rt], writes=[rstd])
            return rstd

        try:
          if stopped:
            kb.stop = True
          with ExitStack() as esA:
              kb.es = esA
              w_out = kb.sb("w_out", [128, KC, D], BF16)
              rgw = kb.sb("rgw", [128, 16, 128], F32)
              wout_v = wout_d.rearrange("(kc p) n -> p kc n", p=128)
              for kc in range(KC):
                  dma("pool", w_out[:, kc, :], wout_v[:, kc, :], "w_out", writes=[w_out], join=(kc > 0))
              dma("sp", rgw[:], rgw_d.rearrange("p (a b) -> p a b", a=16), "rgw", writes=[rgw])

              xt_ring = [kb.sb(f"xt{i}", [128, D], F32) for i in range(2)]
              stats_ring = [[kb.sb(f"st{i}_{j}", [128, 1], F32) for j in range(3)] for i in range(2)]
              hm = kb.sb("hm", [128, KC, SEQS], BF16)
              qT = kb.sb("qT", [64, 8, SEQS], BF16)
              kT = kb.sb("kT", [64, 2, SEQS], BF16)
              vz = kb.sb("vz", [128, 8, 2, 256], BF16)
              cvz = kb.sb("cvz", [128, 2, 2, 256], BF16)
              ckT = kb.sb("ckT", [64, 2, 256], BF16)
              kvf = [kb.sb(f"kvf{i}", [128, 256], F32) for i in range(2)]
              PT = [kb.sb(f"PT{i}", [128, 2, 640], BF16) for i in range(2)]
              rt1 = kb.sb("rt1", [64, 512], F32)
              rt2 = kb.sb("rt2", [64, 512], F32)
              xr = kb.sb("xr", [128, 4, SEQS + 3], F32)
              gy = kb.sb("gy", [128, 4, SEQS], BF16)
              gtm = [kb.sb(f"gtm{i}", [128, 512], F32) for i in range(3)]
              xc = kb.sb("xc", [128, SEQS], F32)
              rg_r = kb.sb("rg_r", [128, 512], F32)
              rg_t = kb.sb("rg_t", [128, 512], F32)
              rg_g = kb.sb("rg_g", [128, 512], F32)
              hdir = [kb.sb(f"hdir{i}", [128, SEQS], F32) for i in range(2)]
              rnn_o = kb.sb("rnn_o", [128, 32], F32)
              rnn_t = kb.sb("rnn_t", [32, 128], F32)
              den_r = kb.sb("den_r", [128, 128], F32)

              op("pool", lambda e: e.memset(vz[:], 0), writes=[vz])
              op("pool", lambda e: e.memset(cvz[:], 0), writes=[cvz])
              op("pool", lambda e: e.memset(xr[:], 0), writes=[xr])
              dma("pool", ckT[:], ckT_d, "ckT", writes=[ckT])
              cv_v = cv_d.rearrange("(kt p) (k d) -> p kt k d", p=128, k=2)
              for kvh in range(2):
                  for var in range(2):
                      dma("pool", cvz[:, :, kvh, var * 128 + var * 64: var * 128 + var * 64 + 64],
                          cv_v[:, :, kvh, :], "cvz", writes=[cvz], join=True)

              def mixer_segment(x_src, tok0, S, kind, seq_idx):
                  ntile = S // 128
                  is_s = (kind == "s")
                  for tl in range(ntile):
                      xt = xt_ring[tl % 2]
                      stt = stats_ring[tl % 2]
                      r0 = tok0 + tl * 128
                      dma("sp", xt[:], x_src[r0:r0 + 128, :], f"xt{tl % 2}", writes=[xt])
                      rstd = norm_stats(xt, tl, stt)
                      op("dve", lambda e, xt=xt, rstd=rstd: e.scalar_tensor_tensor(
                          out=xt[:], in0=xt[:], scalar=rstd[:, 0:1], in1=Gb[:], op0=ALU.mult, op1=ALU.mult),
                         reads=[rstd, Gb], writes=[xt])
                      op("dve", lambda e, xt=xt: e.tensor_tensor(out=xt[:], in0=xt[:], in1=SHb[:], op=ALU.add),
                         reads=[SHb], writes=[xt])
                      for kc in range(KC):
                          op("pe", lambda e, xt=xt, kc=kc: e.transpose(
                              psT[:, kc * 128:(kc + 1) * 128], xt[:, kc * 128:(kc + 1) * 128], ident[:]),
                             reads=[xt, ident], writes=[psT])
                      op("act", lambda e, tl=tl: e.activation(
                          out=hm[:, :, tl * 128:(tl + 1) * 128],
                          in_=psT[:].rearrange("p (k t) -> p k t", k=KC), func=AF.Copy),
                         reads=[psT], writes=[hm])
                  if seq_idx == 0 and not is_s:
                      dbg("Gb", Gb, Gb[:], [128, D])
                      dbg("SHb", SHb, SHb[:], [128, D])
                      dbg("GAb", GAb, GAb[:], [128, D])
                      dbg("xt1", xt_ring[1], xt_ring[1][:], [128, D])
                      dbg("rstd1", stats_ring[1][2], stats_ring[1][2][:], [128, 1])
                      dbg("hm", hm, hm[:, :, 0:256], [128, KC, 256], BF16)
                  _chk("norm")
                  NB = min(512, S)
                  pidx = [0]

                  def proj(cols_ap_fn, M, n0, wbuf=w_in):
                      pp = psP[pidx[0] % 2]
                      pidx[0] += 1
                      for kc in range(KC):
                          op("pe", lambda e, kc=kc, pp=pp: e.matmul(
                              pp[0:M, 0:NB], lhsT=cols_ap_fn(kc), rhs=hm[:, kc, n0:n0 + NB],
                              start=(kc == 0), stop=(kc == KC - 1)),
                             reads=[wbuf, hm], writes=[pp])
                      return pp

                  for nb in range(S // NB):
                      n0 = nb * NB
                      for g in range(10):
                          c0 = g * 64
                          dst = qT[:, g, n0:n0 + NB] if g < 8 else kT[:, g - 8, n0:n0 + NB]
                          dbuf = qT if g < 8 else kT
                          pp = proj(lambda kc, c0=c0: w_in[:, kc, c0:c0 + 64], 64, n0)
                          if not is_s:
                              op("act", lambda e, pp=pp, dst=dst: e.activation(out=dst, in_=pp[0:64, 0:NB], func=AF.Copy),
                                 reads=[pp], writes=[dbuf])
                          else:
                              pos0 = n0
                              pp2 = proj(lambda kc, c0=c0: wsw[:, kc, c0:c0 + 64], 64, n0, wbuf=wsw)
                              op("dve", lambda e, pp=pp: e.tensor_tensor(
                                  out=rt1[:, 0:NB], in0=pp[0:64, 0:NB], in1=cosT[:, pos0:pos0 + NB], op=ALU.mult),
                                 reads=[pp, cosT], writes=[rt1])
                              op("dve", lambda e, pp2=pp2: e.tensor_tensor(
                                  out=rt2[:, 0:NB], in0=pp2[0:64, 0:NB], in1=sinT[:, pos0:pos0 + NB], op=ALU.mult),
                                 reads=[pp2, sinT], writes=[rt2])
                              op("dve", lambda e, dst=dst: e.tensor_tensor(
                                  out=dst, in0=rt1[:, 0:NB], in1=rt2[:, 0:NB], op=ALU.add),
                                 reads=[rt1, rt2], writes=[dbuf])
                      _chk("projqk")
                      for ch in range(4):
                          c0 = 768 + ch * 128
                          pp = proj(lambda kc, c0=c0: w_in[:, kc, c0:c0 + 128], 128, n0)
                          op("act", lambda e, pp=pp, ch=ch: e.activation(
                              out=xr[:, ch, 2 + n0:2 + n0 + NB], in_=pp[:, 0:NB], func=AF.Copy),
                             reads=[pp], writes=[xr])
                      _chk("projxr")
                      for ch in range(4):
                          c0 = 1280 + ch * 128
                          pp = proj(lambda kc, c0=c0: w_in[:, kc, c0:c0 + 128], 128, n0)
                          g0, g1, g2 = gtm
                          op("act", lambda e, pp=pp: e.activation(out=g0[:, 0:NB], in_=pp[:, 0:NB], func=AF.Square),
                             reads=[pp], writes=[g0])
                          op("dve", lambda e: e.tensor_scalar(out=g0[:, 0:NB], in0=g0[:, 0:NB], scalar1=0.044715,
                                                              scalar2=1.0, op0=ALU.mult, op1=ALU.add),
                             reads=[], writes=[g0])
                          op("dve", lambda e, pp=pp: e.tensor_tensor(out=g1[:, 0:NB], in0=g0[:, 0:NB], in1=pp[:, 0:NB],
                                                                    op=ALU.mult),
                             reads=[g0, pp], writes=[g1])
                          op("act", lambda e: e.activation(out=g2[:, 0:NB], in_=g1[:, 0:NB], func=AF.Sigmoid,
                                                           scale=2.0 * GELU_C),
                             reads=[g1], writes=[g2])
                          op("dve", lambda e, pp=pp, ch=ch: e.tensor_tensor(
                              out=gy[:, ch, n0:n0 + NB], in0=g2[:, 0:NB], in1=pp[:, 0:NB], op=ALU.mult),
                             reads=[g2, pp], writes=[gy])
                      _chk("projgy")
                      for tl in range(NB // 128):
                          t_abs = n0 // 128 + tl
                          c_tok = n0 + tl * 128
                          for kc in range(KC):
                              op("pe", lambda e, kc=kc, c_tok=c_tok: e.matmul(
                                  psK[:, 0:256], lhsT=hm[:, kc, c_tok:c_tok + 128], rhs=w_in[:, kc, 512:768],
                                  start=(kc == 0), stop=(kc == KC - 1)),
                                 reads=[w_in, hm], writes=[psK])
                          _chk("kv1")
                          for var in range(2):
                              op("act", lambda e, var=var, t_abs=t_abs: e.activation(
                                  out=vz[:, t_abs, :, var * 192: var * 192 + 64],
                                  in_=psK[:, 128:256].rearrange("p (k d) -> p k d", k=2), func=AF.Copy),
                                 reads=[psK], writes=[vz])
                          _chk("kv2")
                          if not is_s:
                              kf = kvf[t_abs % 2]
                              op("act", lambda e, kf=kf: e.activation(out=kf[:], in_=psK[:, 0:256], func=AF.Copy),
                                 reads=[psK], writes=[kf])
                              _chk("kv3")
                              r0 = tok0 + t_abs * 128
                              dma("sp", nk_d[r0:r0 + 128, :], kf[:, 0:128], f"o_nk{t_abs % 2}", reads=[kf])
                              dma("sp", nv_d[r0:r0 + 128, :], kf[:, 128:256], f"o_nv{t_abs % 2}", reads=[kf])

                  _chk("proj")
                  def rg_section():
                      HB = min(512, S)
                      nh = S // HB
                      for ch in range(4):
                          yield op("dve", lambda e, ch=ch: e.tensor_scalar(
                              out=xc[:, 0:S], in0=xr[:, ch, 0:S], scalar1=fvec[:, ch * 4:ch * 4 + 1],
                              scalar2=fvec[:, 16 + ch:17 + ch], op0=ALU.mult, op1=ALU.add),
                             reads=[xr, fvec], writes=[xc])
                          for jt in range(1, 4):
                              yield op("dve", lambda e, ch=ch, jt=jt: e.scalar_tensor_tensor(
                                  out=xc[:, 0:S], in0=xr[:, ch, jt:jt + S], scalar=fvec[:, ch * 4 + jt:ch * 4 + jt + 1],
                                  in1=xc[:, 0:S], op0=ALU.mult, op1=ALU.add),
                                 reads=[xr, fvec], writes=[xc])
                          for dr in range(2):
                              hd = hdir[dr]
                              order = list(range(nh)) if dr == 0 else list(range(nh - 1, -1, -1))
                              for oi, hb_ in enumerate(order):
                                  c0 = hb_ * HB
                                  pa, pi_ = psP[0], psP[1]
                                  yield op("pe", lambda e, c0=c0, dr=dr, ch=ch: e.matmul(
                                      pa[:, 0:HB], lhsT=rgw[:, (0 * 2 + dr) * 4 + ch, :], rhs=xc[:, c0:c0 + HB],
                                      start=True, stop=True), reads=[rgw, xc], writes=[pa])
                                  yield op("pe", lambda e, c0=c0, dr=dr, ch=ch: e.matmul(
                                      pi_[:, 0:HB], lhsT=rgw[:, (1 * 2 + dr) * 4 + ch, :], rhs=xc[:, c0:c0 + HB],
                                      start=True, stop=True), reads=[rgw, xc], writes=[pi_])
                                  fi = dr * 4 + ch
                                  yield op("act", lambda e, fi=fi: e.activation(
                                      out=rg_r[:, 0:HB], in_=pa[:, 0:HB], func=AF.Sigmoid, bias=fvec[:, 20 + fi:21 + fi]),
                                     reads=[pa, fvec], writes=[rg_r])
                                  yield op("act", lambda e, fi=fi: e.activation(
                                      out=rg_g[:, 0:HB], in_=pi_[:, 0:HB], func=AF.Sigmoid, bias=fvec[:, 28 + fi:29 + fi]),
                                     reads=[pi_, fvec], writes=[rg_g])
                                  yield op("act", lambda e, fi=fi: e.activation(
                                      out=rg_t[:, 0:HB], in_=rg_r[:, 0:HB], func=AF.Exp, scale=rgs[:, 8 + fi:9 + fi]),
                                     reads=[rg_r, rgs], writes=[rg_t])
                                  yield op("act", lambda e, fi=fi: e.activation(
                                      out=rg_r[:, 0:HB], in_=rg_r[:, 0:HB], func=AF.Exp, scale=rgs[:, fi:fi + 1]),
                                     reads=[rgs], writes=[rg_r])
                                  yield op("dve", lambda e: e.tensor_scalar(
                                      out=rg_t[:, 0:HB], in0=rg_t[:, 0:HB], scalar1=1.0, scalar2=-1.0, op0=ALU.min,
                                      op1=ALU.mult), reads=[], writes=[rg_t])
                                  yield op("act", lambda e: e.activation(
                                      out=rg_t[:, 0:HB], in_=rg_t[:, 0:HB], func=AF.Sqrt, scale=1.0, bias=1.0),
                                     reads=[], writes=[rg_t])
                                  yield op("dve", lambda e: e.tensor_tensor(out=rg_g[:, 0:HB], in0=rg_g[:, 0:HB], in1=rg_t[:, 0:HB],
                                                                      op=ALU.mult), reads=[rg_t], writes=[rg_g])
                                  yield op("dve", lambda e, c0=c0: e.tensor_tensor(
                                      out=rg_g[:, 0:HB], in0=rg_g[:, 0:HB], in1=xc[:, c0:c0 + HB], op=ALU.mult),
                                     reads=[xc], writes=[rg_g])
                                  if oi == 0:
                                      init = stT[:, dr * 4 + ch:dr * 4 + ch + 1] if is_s else 0.0
                                      rd_init = [stT] if is_s else []
                                  else:
                                      if dr == 0:
                                          init = hd[:, c0 - 1:c0]
                                      else:
                                          init = hd[:, c0 + HB:c0 + HB + 1]
                                      rd_init = []
                                  if dr == 0:
                                      yield op("dve", lambda e, hd=hd, c0=c0, init=init: e.tensor_tensor_scan(
                                          out=hd[:, c0:c0 + HB], data0=rg_r[:, 0:HB], data1=rg_g[:, 0:HB], initial=init,
                                          op0=ALU.mult, op1=ALU.add),
                                         reads=[rg_r, rg_g] + rd_init, writes=[hd])
                                  else:
                                      yield op("dve", lambda e, hd=hd, c0=c0, init=init: e.tensor_tensor_scan(
                                          out=hd[:, c0:c0 + HB][:, ::-1], data0=rg_r[:, 0:HB][:, ::-1],
                                          data1=rg_g[:, 0:HB][:, ::-1], initial=init, op0=ALU.mult, op1=ALU.add),
                                         reads=[rg_r, rg_g] + rd_init, writes=[hd])
                          if not is_s:
                              c_f = (seq_idx * 2 + 0) * 4 + ch
                              c_b = (seq_idx * 2 + 1) * 4 + ch
                              yield op("act", lambda e, c_f=c_f: e.activation(out=rnn_o[:, c_f:c_f + 1], in_=hdir[0][:, S - 1:S],
                                                                        func=AF.Copy), reads=[hdir[0]], writes=[rnn_o])
                              yield op("act", lambda e, c_b=c_b: e.activation(out=rnn_o[:, c_b:c_b + 1], in_=hdir[1][:, 0:1],
                                                                        func=AF.Copy), reads=[hdir[1]], writes=[rnn_o])
                          yield op("dve", lambda e: e.tensor_tensor(out=hdir[0][:, 0:S], in0=hdir[0][:, 0:S], in1=hdir[1][:, 0:S],
                                                              op=ALU.add), reads=[hdir[1]], writes=[hdir[0]])
                          yield op("dve", lambda e, ch=ch: e.tensor_tensor(out=hm[:, 4 + ch, 0:S], in0=hdir[0][:, 0:S],
                                                                    in1=gy[:, ch, 0:S], op=ALU.mult),
                             reads=[hdir[0], gy], writes=[hm])


                  rgen = rg_section()

                  def rg_pull(n):
                      for _ in range(n):
                          next(rgen, None)

                  nblk = S // 128
                  pti = [0]
                  for b in range(nblk):
                      if is_s:
                          slots = [kt for kt in (b - 1, b, b + 1)]
                      else:
                          slots = list(range(nblk))
                      for j in range(4):
                          kvh = j // 2
                          P = PT[pti[0] % 2]
                          pti[0] += 1
                          rg_pull(6)
                          for hh in range(2):
                              h = 2 * j + hh
                              pb = psB[hh]
                              lo, hi = None, None
                              for si, kt in enumerate(slots):
                                  if kt < 0 or kt >= nblk:
                                      continue
                                  if lo is None:
                                      lo = si
                                  hi = si + 1
                                  op("pe", lambda e, pb=pb, si=si, kt=kt, h=h: e.matmul(
                                      pb[:, si * 128:(si + 1) * 128], lhsT=kT[:, kvh, kt * 128:(kt + 1) * 128],
                                      rhs=qT[:, h, b * 128:(b + 1) * 128], start=True, stop=True),
                                     reads=[kT, qT], writes=[pb])
                              op("act", lambda e, pb=pb, hh=hh, P=P, lo=lo, hi=hi: e.activation(
                                  out=P[:, hh, lo * 128:hi * 128], in_=pb[:, lo * 128:hi * 128], func=AF.Exp,
                                  scale=ATTN_SCALE),
                                 reads=[pb], writes=[P])
                              if is_s:
                                  for ct in range(2):
                                      op("pe", lambda e, ct=ct, hh=hh, h=h: e.matmul(
                                          psC[:, (hh * 2 + ct) * 128:(hh * 2 + ct + 1) * 128],
                                          lhsT=ckT[:, kvh, ct * 128:(ct + 1) * 128],
                                          rhs=qT[:, h, b * 128:(b + 1) * 128], start=True, stop=True),
                                         reads=[ckT, qT], writes=[psC])
                          if is_s:
                              op("act", lambda e, P=P: e.activation(
                                  out=P[:, :, 384:640], in_=psC[:].rearrange("p (h c) -> p h c", h=2), func=AF.Exp,
                                  scale=ATTN_SCALE),
                                 reads=[psC], writes=[P])
                              lo = 1 if b == 0 else 0
                              hi = 2 if b == nblk - 1 else 3
                              op("dve", lambda e, P=P, lo=lo, hi=hi: e.tensor_tensor(
                                  out=P[:, :, lo * 128:hi * 128], in0=P[:, :, lo * 128:hi * 128],
                                  in1=mask3[:, lo * 128:hi * 128].unsqueeze(1).to_broadcast([128, 2, (hi - lo) * 128]),
                                  op=ALU.mult),
                                 reads=[mask3], writes=[P])
                          terms = []
                          rg_pull(6)
                          for hh in range(2):
                              for si, kt in enumerate(slots):
                                  if kt < 0 or kt >= nblk:
                                      continue
                                  terms.append((hh, si, vz, kt))
                              if is_s:
                                  for ct in range(2):
                                      terms.append((hh, 3 + ct, cvz, ct))
                          rg_pull(4)
                          for which in range(2):
                              for ti, (hh, si, vsrc, kt) in enumerate(terms):
                                  if which == 0:
                                      lhs = vsrc[:, kt, kvh, hh * 128:(hh + 1) * 128]
                                      rd = [vsrc, P]
                                  else:
                                      lhs = onesv[:, hh * 128:(hh + 1) * 128]
                                      rd = [onesv, P]
                                  op("pe", lambda e, lhs=lhs, hh=hh, si=si, which=which, ti=ti, P=P: e.matmul(
                                      psK[:, 256 + which * 128:256 + (which + 1) * 128], lhsT=lhs,
                                      rhs=P[:, hh, si * 128:(si + 1) * 128],
                                      start=(ti == 0), stop=(ti == len(terms) - 1)),
                                     reads=rd, writes=[psND])
                          op("dve", lambda e, j=j: e.tensor_scalar(
                              out=den_r[:], in0=psK[:, 384:512], scalar1=esink[:, j:j + 1], scalar2=None, op0=ALU.add),
                             reads=[psND, esink], writes=[den_r])
                          op("dve", lambda e: e.reciprocal(out=den_r[:], in_=den_r[:]), reads=[], writes=[den_r])
                          op("dve", lambda e, j=j, b=b: e.tensor_tensor(
                              out=hm[:, j, b * 128:(b + 1) * 128], in0=psK[:, 256:384], in1=den_r[:], op=ALU.mult),
                             reads=[psND, den_r], writes=[hm])

                  for _ in rgen:
                      pass
                  _chk("attn")
                  if is_s:
                      dbg("mixs", hm, hm[:], [128, KC, SEQS], BF16)
                  elif seq_idx == 0:
                      dbg("mixp0", hm, hm[:, :, 0:256], [128, KC, 256], BF16)
                  _chk("rg")
                  for tl in range(ntile):
                      xt = xt_ring[tl % 2]
                      r0 = tok0 + tl * 128
                      dma("sp", xt[:], x_src[r0:r0 + 128, :], f"xt{tl % 2}", writes=[xt])
                      for nbk in range(2):
                          for mc in range(KC):
                              op("pe", lambda e, nbk=nbk, mc=mc, tl=tl: e.matmul(
                                  psT[:, nbk * 512:(nbk + 1) * 512], lhsT=hm[:, mc, tl * 128:(tl + 1) * 128],
                                  rhs=w_out[:, mc, nbk * 512:(nbk + 1) * 512], start=(mc == 0), stop=(mc == KC - 1)),
                                 reads=[hm, w_out], writes=[psT])
                      op("dve", lambda e: e.tensor_tensor(out=gtmp[:], in0=psT[:], in1=GAb[:], op=ALU.mult),
                         reads=[psT, GAb], writes=[gtmp])
                      op("dve", lambda e, xt=xt: e.tensor_tensor(out=xt[:], in0=xt[:], in1=gtmp[:], op=ALU.add),
                         reads=[gtmp], writes=[xt])
                      g0 = (1024 if is_s else 0) + r0
                      dma("sp", x1_d[g0:g0 + 128, :], xt[:], f"x1_out{tl % 2}", reads=[xt])

              load_mods(0, "A")
              for s in range(4):
                  mixer_segment(xp_d, s * SEQP, SEQP, "p", s)
                  if s == 0:
                      conv_chunks(32)
              op("pe", lambda e: e.transpose(psT[0:32, 0:128], rnn_o[:, 0:32], ident[:]),
                 reads=[rnn_o, ident], writes=[psT])
              op("act", lambda e: e.activation(out=rnn_t[:], in_=psT[0:32, 0:128], func=AF.Copy),
                 reads=[psT], writes=[rnn_t])
              dma("sp", rn_d, rnn_t[:], "o_rn", reads=[rnn_t])
              load_mods(1, "A")
              mixer_segment(xs_d, 0, SEQS, "s", 0)
              kb.es = es
        except StopBuild:
            stopped = True
            kb.es = es
        x1_done = [("x1_out0", kb.cnt.get("x1_out0", 0)), ("x1_out1", kb.cnt.get("x1_out1", 0))]
        if DEBUG and not kb.stop:
            dx = nc.dram_tensor("dbg_x1", [2048, D], F32, kind="ExternalOutput").ap()
            dma("sp", dx, x1_d, "dbg", extra=x1_done)

        if STAGE_B and not kb.stop:
            with ExitStack() as esB:
                kb.es = esB
                kb.barrier(dma_sems=[k_ for k_ in kb.cnt if k_ not in kb.engs])
                skT = kb.sb("skT", [128, 16, 128], BF16)
                GFb = kb.sb("GFb", [128, D], F32)
                wq_v = wq_d.rearrange("(kc p) n -> p kc n", p=128)
                for kc in range(KC):
                    dma("pool", wq[:, kc, :], wq_v[:, kc, :], "wq", writes=[wq], join=(kc > 0))
                dma("pool", skT[:], skT_d.rearrange("p (a b) -> p a b", a=16), "skT", writes=[skT])
                dma("sp", GFb[:], gfin_d.to_broadcast([128, D]), "gfb", writes=[GFb])

                x1t = [kb.sb(f"x1t{i}", [128, D], F32) for i in range(2)]
                h2 = [kb.sb(f"h2_{i}", [128, D], F32) for i in range(2)]
                h2b = [kb.sb(f"h2b_{i}", [128, D], BF16) for i in range(2)]
                stB = [[kb.sb(f"stB{i}_{j}", [128, 1], F32) for j in range(3)] for i in range(2)]
                h2T = kb.sb("h2T", [128, KC, 128], BF16)
                qpT = kb.sb("qpT", [128, 16, 128], BF16)
                ssb = kb.sb("ssb", [128, 16, 128], F32)
                v16 = kb.sb("v16", [128, 16, 16], F32)
                i16 = kb.sb("i16", [128, 16, 16], U32)
                i16f = kb.sb("i16f", [128, 16, 16], F32)
                cand = kb.sb("cand", [128, 8, 256], F32)
                best = kb.sb("best", [128, 8, 16], F32)
                pos = kb.sb("pos", [128, 8, 16], U32)
                pif = kb.sb("pif", [128, 8, 16], F32)
                pjf = kb.sb("pjf", [128, 8, 16], F32)
                pit = kb.sb("pit", [128, 8, 16], U32)
                eq = kb.sb("eq", [128, 8, 256], F32)
                cand2 = eq
                isel = kb.sb("isel", [128, 8, 16], F32)
                jsel = kb.sb("jsel", [128, 8, 16], F32)
                eidf = kb.sb("eidf", [128, 128], F32)
                eid = [kb.sb(f"eid{i}", [128, 128], I32) for i in range(2)]
                gwb = [kb.sb(f"gw{i}", [128, 8, 16], F32) for i in range(2)]
                gsum = kb.sb("gsum", [128, 8], F32)
                conv_chunks(32)
                gb = [kb.sb(f"gb{i}", [128, 2 * D], BF16) for i in range(NGB)]
                dg = [kb.sb(f"dg{i}", [128, 128], BF16) for i in range(4)]
                NGRP = 128 // GS
                av_ = [kb.sb(f"av{i}", [128, GS], F32) for i in range(2)]
                sq_ = [kb.sb(f"sq{i}", [128, GS], F32) for i in range(2)]
                zz_ = [kb.sb(f"zz{i}", [128, GS], F32) for i in range(2)]
                xg_ = [kb.sb(f"xg{i}", [128, GS], F32) for i in range(2)]
                sg_ = [kb.sb(f"sg{i}", [128, GS], F32) for i in range(2)]
                wg_ = [kb.sb(f"wg{i}", [128, GS], F32) for i in range(2)]
                di_ = [0]
                yt = kb.sb("yt", [128, D], F32)
                stF = [kb.sb(f"stF{j}", [128, 1], F32) for j in range(3)]
                gi_ = [0]

                def prologue(t):
                    is_s = t >= NTP
                    xb = x1t[t % 2]
                    hb = h2[t % 2]
                    ei = eid[t % 2]
                    gw = gwb[t % 2]
                    dma("sp", xb[:], x1_d[t * 128:(t + 1) * 128, :], f"x1t{t % 2}", writes=[xb], extra=x1_done)
                    rstd = norm_stats(xb, t, stB[t % 2])
                    yield op("dve", lambda e: e.scalar_tensor_tensor(
                        out=hb[:], in0=xb[:], scalar=rstd[:, 0:1], in1=Gb[:], op0=ALU.mult, op1=ALU.mult),
                       reads=[xb, rstd, Gb], writes=[hb])
                    yield op("dve", lambda e: e.tensor_tensor(out=hb[:], in0=hb[:], in1=SHb[:], op=ALU.add),
                       reads=[SHb], writes=[hb])
                    op("act", lambda e: e.activation(out=h2b[t % 2][:], in_=hb[:], func=AF.Copy),
                       reads=[hb], writes=[h2b[t % 2]])
                    for kc in range(KC):
                        op("pe", lambda e, kc=kc: e.transpose(
                            psT[:, kc * 128:(kc + 1) * 128], hb[:, kc * 128:(kc + 1) * 128], ident[:]),
                           reads=[hb, ident], writes=[psT])
                    op("act", lambda e: e.activation(out=h2T[:], in_=psT[:].rearrange("p (k t) -> p k t", k=KC),
                                                     func=AF.Copy), reads=[psT], writes=[h2T])
                    qbanks = [psP[0], psP[1], psK, psC]
                    for c in range(16):
                        pq = qbanks[c // 4]
                        for kc in range(KC):
                            op("pe", lambda e, c=c, kc=kc, pq=pq: e.matmul(
                                pq[:, (c % 4) * 128:(c % 4 + 1) * 128], lhsT=wq[:, kc, c * 128:(c + 1) * 128],
                                rhs=h2T[:, kc, :], start=(kc == 0), stop=(kc == KC - 1)),
                               reads=[wq, h2T], writes=[pq])
                        if c % 4 == 3:
                            op("act", lambda e, c=c, pq=pq: e.activation(
                                out=qpT[:, c - 3:c + 1, :], in_=pq[:].rearrange("p (a b) -> p a b", a=4), func=AF.Copy),
                               reads=[pq], writes=[qpT])
                    sbanks = [psT, psT, psP[0], psP[1]]
                    for c in range(16):
                        bk = sbanks[c // 4]
                        off = (c // 4) * 512 if c // 4 < 2 else 0
                        op("pe", lambda e, c=c, bk=bk, off=off: e.matmul(
                            bk[:, off + (c % 4) * 128:off + (c % 4 + 1) * 128], lhsT=qpT[:, c, :], rhs=skT[:, c, :],
                            start=True, stop=True), reads=[qpT, skT], writes=[bk])
                    op("act", lambda e: e.activation(out=ssb[:, 0:8, :], in_=psT[:].rearrange("p (a b) -> p a b", a=8),
                                                     func=AF.Copy), reads=[psT], writes=[ssb])
                    op("act", lambda e: e.activation(out=ssb[:, 8:12, :], in_=psP[0][:].rearrange("p (a b) -> p a b", a=4),
                                                     func=AF.Copy), reads=[psP[0]], writes=[ssb])
                    op("act", lambda e: e.activation(out=ssb[:, 12:16, :], in_=psP[1][:].rearrange("p (a b) -> p a b", a=4),
                                                     func=AF.Copy), reads=[psP[1]], writes=[ssb])
                    for _ in range(24):
                        yield None
                    for c in range(16):
                        yield op("dve", lambda e, c=c: e.max(out=v16[:, c, 0:8], in_=ssb[:, c, :]), reads=[ssb], writes=[v16])
                        yield op("dve", lambda e, c=c: e.max_index(out=i16[:, c, 0:8], in_max=v16[:, c, 0:8],
                                                             in_values=ssb[:, c, :]), reads=[v16, ssb], writes=[i16])
                        yield op("dve", lambda e, c=c: e.match_replace(out=ssb[:, c, :], in_to_replace=v16[:, c, 0:8],
                                                                 in_values=ssb[:, c, :], imm_value=-1e30),
                           reads=[v16], writes=[ssb])
                        yield op("dve", lambda e, c=c: e.max(out=v16[:, c, 8:16], in_=ssb[:, c, :]), reads=[ssb], writes=[v16])
                        yield op("dve", lambda e, c=c: e.max_index(out=i16[:, c, 8:16], in_max=v16[:, c, 8:16],
                                                             in_values=ssb[:, c, :]), reads=[v16, ssb], writes=[i16])
                    yield op("dve", lambda e: e.tensor_copy(out=i16f[:], in_=i16[:]), reads=[i16], writes=[i16f])
                    v16v = v16[:].rearrange("p (h s) k -> p h s k", s=2)
                    yield op("dve", lambda e: e.tensor_tensor(
                        out=cand[:].rearrange("p h (i j) -> p h i j", i=16),
                        in0=v16v[:, :, 0, :].unsqueeze(3).to_broadcast([128, 8, 16, 16]),
                        in1=v16v[:, :, 1, :].unsqueeze(2).to_broadcast([128, 8, 16, 16]), op=ALU.add),
                       reads=[v16], writes=[cand])
                    for h in range(8):
                        yield op("dve", lambda e, h=h: e.max(out=best[:, h, 0:8], in_=cand[:, h, :]), reads=[cand], writes=[best])
                        yield op("dve", lambda e, h=h: e.max_index(out=pos[:, h, 0:8], in_max=best[:, h, 0:8],
                                                             in_values=cand[:, h, :]), reads=[best, cand], writes=[pos])
                        yield op("dve", lambda e, h=h: e.match_replace(out=cand2[:, h, :], in_to_replace=best[:, h, 0:8],
                                                                 in_values=cand[:, h, :], imm_value=-1e30),
                           reads=[best, cand], writes=[cand2])
                        yield op("dve", lambda e, h=h: e.max(out=best[:, h, 8:16], in_=cand2[:, h, :]),
                           reads=[cand2], writes=[best])
                        yield op("dve", lambda e, h=h: e.max_index(out=pos[:, h, 8:16], in_max=best[:, h, 8:16],
                                                             in_values=cand2[:, h, :]), reads=[best, cand2], writes=[pos])
                    yield op("dve", lambda e: e.tensor_single_scalar(out=pit[:], in_=pos[:], scalar=4,
                                                               op=ALU.logical_shift_right), reads=[pos], writes=[pit])
                    yield op("dve", lambda e: e.tensor_copy(out=pif[:], in_=pit[:]), reads=[pit], writes=[pif])
                    yield op("dve", lambda e: e.tensor_single_scalar(out=pit[:], in_=pos[:], scalar=15,
                                                               op=ALU.bitwise_and), reads=[pos], writes=[pit])
                    yield op("dve", lambda e: e.tensor_copy(out=pjf[:], in_=pit[:]), reads=[pit], writes=[pjf])
                    i16v = i16f[:].rearrange("p (h s) k -> p h s k", s=2)
                    iob = iota16[:].unsqueeze(1).unsqueeze(1).to_broadcast([128, 8, 16, 16])
                    for (pf, half, dst) in ((pif, 0, isel), (pjf, 1, jsel)):
                        yield op("dve", lambda e, pf=pf: e.tensor_tensor(
                            out=eq[:].rearrange("p h (i j) -> p h i j", i=16), in0=pf[:].unsqueeze(3).to_broadcast([128, 8, 16, 16]), in1=iob, op=ALU.is_equal),
                           reads=[pf, iota16], writes=[eq])
                        yield op("dve", lambda e, half=half: e.tensor_tensor(
                            out=eq[:].rearrange("p h (i j) -> p h i j", i=16), in0=eq[:].rearrange("p h (i j) -> p h i j", i=16), in1=i16v[:, :, half, :].unsqueeze(2).to_broadcast([128, 8, 16, 16]),
                            op=ALU.mult), reads=[i16f], writes=[eq])
                        yield op("dve", lambda e, dst=dst: e.tensor_reduce(out=dst[:], in_=eq[:].rearrange("p h (i j) -> p h i j", i=16), axis=AX.X, op=ALU.add),
                           reads=[eq], writes=[dst])
                    yield op("dve", lambda e: e.scalar_tensor_tensor(
                        out=eidf[:], in0=isel[:].rearrange("p h k -> p (h k)"), scalar=128.0,
                        in1=jsel[:].rearrange("p h k -> p (h k)"), op0=ALU.mult, op1=ALU.add),
                       reads=[isel, jsel], writes=[eidf])
                    yield op("dve", lambda e: e.tensor_copy(out=ei[:], in_=eidf[:]), reads=[eidf], writes=[ei])
                    yield op("dve", lambda e: e.tensor_tensor(
                        out=gw[:], in0=best[:], in1=best[:, :, 0:1].to_broadcast([128, 8, 16]), op=ALU.subtract),
                       reads=[best], writes=[gw])
                    op("act", lambda e: e.activation(out=gw[:], in_=gw[:], func=AF.Exp), reads=[], writes=[gw])
                    yield op("dve", lambda e: e.tensor_reduce(out=gsum[:], in_=gw[:], axis=AX.X, op=ALU.add),
                       reads=[gw], writes=[gsum])
                    yield op("dve", lambda e: e.reciprocal(out=gsum[:], in_=gsum[:]), reads=[], writes=[gsum])
                    yield op("dve", lambda e: e.tensor_tensor(
                        out=gw[:], in0=gw[:], in1=gsum[:].unsqueeze(2).to_broadcast([128, 8, 16]), op=ALU.mult),
                       reads=[gsum], writes=[gw])
                def body(t, gen):
                    is_s = t >= NTP
                    xb = x1t[t % 2]
                    hb = h2[t % 2]
                    ei = eid[t % 2]
                    gw = gwb[t % 2]

                    def pull():
                        if gen is not None:
                            next(gen, None)
                    gwf = gw[:].rearrange("p h k -> p (h k)")
                    slots = {}
                    pending = []

                    def finish_group(grp):
                        par = grp % 2
                        av, xg, sg, wg = av_[par], xg_[par], sg_[par], wg_[par]
                        op("dve", lambda e: e.tensor_tensor(out=wg[:], in0=sg[:], in1=xg[:], op=ALU.mult),
                           reads=[sg, xg], writes=[wg])
                        for k in range(GS):
                            c = grp * GS + k
                            dgb = dg[di_[0] % 4]
                            di_[0] += 1
                            op("act", lambda e: e.activation(out=dgb[:], in_=ident[:], func=AF.Copy, scale=wg[:, k:k + 1]),
                               reads=[wg, ident], writes=[dgb])
                            for half in range(2):
                                op("pe", lambda e: e.matmul(
                                    psB[half][:, 0:512], lhsT=dgb[:], rhs=slots[c][:, D + half * 512:D + (half + 1) * 512],
                                    start=(c == 0), stop=(c == 127)), reads=[dgb, slots[c]], writes=[psB[half]])

                    for grp in range(NGRP):
                        par = grp % 2
                        av, sq, zz, xg, sg = av_[par], sq_[par], zz_[par], xg_[par], sg_[par]
                        for k in range(GS):
                            c = grp * GS + k
                            g = gb[gi_[0] % NGB]
                            sname = f"gb{gi_[0] % NGB}"
                            gi_[0] += 1
                            slots[c] = g
                            kb.gather(g[:], uv16, ei[:, c:c + 1], sname, reads=[ei, uvbuf], writes=[g])
                            op("dve", lambda e: e.scalar_tensor_tensor(
                                out=junk_dve[:], in0=g[:, 0:D], scalar=1.0, in1=h2b[t % 2][:], op0=ALU.mult, op1=ALU.mult,
                                accum_out=av[:, k:k + 1]), reads=[g, h2b[t % 2]], writes=[av])
                            pull()
                        cs = slice(grp * GS, (grp + 1) * GS)
                        op("dve", lambda e: e.tensor_tensor(out=sq[:], in0=av[:], in1=av[:], op=ALU.mult),
                           reads=[av], writes=[sq])
                        op("dve", lambda e: e.tensor_tensor(out=sq[:], in0=sq[:], in1=av[:], op=ALU.mult),
                           reads=[av], writes=[sq])
                        op("dve", lambda e: e.scalar_tensor_tensor(out=zz[:], in0=sq[:], scalar=0.044715, in1=av[:],
                                                                   op0=ALU.mult, op1=ALU.add),
                           reads=[sq, av], writes=[zz])
                        op("act", lambda e: e.activation(out=sg[:], in_=zz[:], func=AF.Sigmoid, scale=2.0 * GELU_C),
                           reads=[zz], writes=[sg])
                        op("dve", lambda e: e.tensor_tensor(out=xg[:], in0=av[:], in1=gwf[:, cs], op=ALU.mult),
                           reads=[av, gw], writes=[xg])
                        if pending:
                            finish_group(pending.pop(0))
                        pending.append(grp)
                    while pending:
                        finish_group(pending.pop(0))
                    if gen is not None:
                        for _ in gen:
                            pass
                    for half in range(2):
                        hs = slice(half * 512, (half + 1) * 512)
                        op("dve", lambda e: e.tensor_tensor(out=yt[:, hs], in0=psB[half][:, 0:512], in1=GAb[:, hs], op=ALU.mult),
                           reads=[psB[half], GAb], writes=[yt])
                    op("dve", lambda e: e.tensor_tensor(out=yt[:], in0=yt[:], in1=xb[:], op=ALU.add),
                       reads=[xb], writes=[yt])
                    rstd3 = norm_stats(yt, t, stF)
                    op("dve", lambda e: e.scalar_tensor_tensor(
                        out=yt[:], in0=yt[:], scalar=rstd3[:, 0:1], in1=GFb[:], op0=ALU.mult, op1=ALU.mult),
                       reads=[rstd3, GFb], writes=[yt])
                    if is_s:
                        dst = ys_d[(t - NTP) * 128:(t - NTP + 1) * 128, :]
                    else:
                        dst = yp_d[t * 128:(t + 1) * 128, :]
                    dma("sp", dst, yt[:], "o_y", reads=[yt])

                load_mods(0, "B")
                for _ in prologue(0):
                    pass
                for t in range(NT):
                    nxt = None
                    if t + 1 < NT:
                        if t + 1 == NTP:
                            load_mods(1, "B", parts=("pro",))
                        nxt = prologue(t + 1)
                    body(t, nxt)
                    if t + 1 == NTP:
                        load_mods(1, "B", parts=("ga",))
                kb.es = es

        for s_ in kb.cnt:
            if kb.cnt[s_] > 0 and s_ != "sp":
                nc.sync.wait_ge(kb.sems[s_], kb.cnt[s_])
    return nc


def _consts():
    ident = np.eye(128, dtype=np.float32)
    pos = np.arange(1024)
    row = (pos // 64).astype(np.float32)
    col = (pos % 64).astype(np.float32)
    inv = (np.float32(10000.0) ** (-np.arange(16, dtype=np.float32) / np.float32(16))).astype(np.float32)
    ang = np.concatenate([row[:, None] * inv, col[:, None] * inv], axis=-1).astype(np.float32)
    c = np.cos(ang).astype(np.float32).T
    s = np.sin(ang).astype(np.float32).T
    cosT = np.concatenate([c, c], axis=0)
    sinT = np.concatenate([-s, s], axis=0)
    ki = np.arange(128)[:, None]
    qi = np.arange(128)[None, :]
    L = (qi <= ki).astype(np.float32)
    U = (ki <= qi).astype(np.float32)
    mask3 = np.concatenate([L, np.ones((128, 128), np.float32), U], axis=1)
    iota16 = np.tile(np.arange(16, dtype=np.float32)[None, :], (128, 1))
    onesv = np.zeros((128, 256), np.float32)
    onesv[:, 0:64] = 1.0
    onesv[:, 192:256] = 1.0
    return dict(ident=ident, cosT=np.ascontiguousarray(cosT), sinT=np.ascontiguousarray(sinT), mask3=mask3,
                iota16=iota16, onesv=onesv)


_NC_CACHE = {}


def kernel(x_prompt, x_sample, cache_k, cache_v, state_rnn, c, c_ctx, w_mod, b_mod,
           g_norm_mix, g_norm_ffn, w_in, conv_w, conv_b, rg_w_a, rg_b_a, rg_w_i, rg_b_i,
           rg_lambda, attn_sink, w_out, peer_w_query, peer_sub_keys, peer_u, peer_v, g_final):
    f = lambda a: np.ascontiguousarray(np.asarray(a, dtype=np.float32))
    x_prompt, x_sample = f(x_prompt), f(x_sample)
    consts = _consts()

    def colT(v):
        v = f(v).reshape(-1, 128)
        return np.ascontiguousarray(v.T)

    fvec = np.zeros((128, 48), np.float32)
    cw = f(conv_w)[0]
    for ch in range(4):
        for jt in range(4):
            fvec[:, ch * 4 + jt] = cw[jt, ch * 128:(ch + 1) * 128]
    fvec[:, 16:20] = colT(f(conv_b)[0])
    fvec[:, 20:28] = colT(f(rg_b_a)[0].reshape(-1))
    fvec[:, 28:36] = colT(f(rg_b_i)[0].reshape(-1))
    fvec[:, 36:44] = colT(f(rg_lambda)[0].reshape(-1))
    rgw = np.zeros((128, 16, 128), np.float32)
    for gate, W in enumerate((f(rg_w_a)[0], f(rg_w_i)[0])):
        for dr in range(2):
            for ch in range(4):
                for bb in range(2):
                    rgw[bb * 64:(bb + 1) * 64, (gate * 2 + dr) * 4 + ch, bb * 64:(bb + 1) * 64] = W[dr, ch * 2 + bb]
    sink = f(attn_sink)[0]
    sinkT = np.zeros((128, 4), np.float32)
    for j in range(4):
        sinkT[0:64, j] = sink[2 * j]
        sinkT[64:128, j] = sink[2 * j + 1]
    skT = np.ascontiguousarray(f(peer_sub_keys)[0].reshape(16, 128, 128).transpose(2, 0, 1)).reshape(128, 16 * 128)
    perm = np.concatenate([np.concatenate([np.arange(h * 64 + 32, h * 64 + 64), np.arange(h * 64, h * 64 + 32)])
                           for h in range(10)])
    w_sw = np.ascontiguousarray(f(w_in)[0][:, perm])
    shared = dict(
        w_sw=w_sw, w_mod=f(w_mod)[0], b_mod=f(b_mod)[0].reshape(1, -1), g_mix=f(g_norm_mix)[0].reshape(1, -1),
        g_ffn=f(g_norm_ffn)[0].reshape(1, -1), g_fin=f(g_final).reshape(1, -1), w_in=f(w_in)[0], fvec=fvec,
        rgw=rgw.reshape(128, -1), sink=sinkT, w_out=f(w_out)[0], w_q=f(peer_w_query)[0].reshape(D, 2048),
        skT=skT, peer_u=f(peer_u)[0], peer_v=f(peer_v)[0], **consts)
    cache_k, cache_v, state_rnn, c, c_ctx = f(cache_k), f(cache_v), f(state_rnn), f(c), f(c_ctx)
    in_maps = []
    for i in range(NCORES):
        cT = np.zeros((128, 16), np.float32)
        cT[:, 0::2] = colT(c_ctx)
        cT[:, 1::2] = colT(c[i])
        m = dict(shared)
        m.update(
            xp=x_prompt[4 * i:4 * i + 4].reshape(1024, D),
            xs=x_sample[i],
            ckT=np.ascontiguousarray(cache_k[i, 0].transpose(2, 1, 0)),
            cv=cache_v[i, 0].reshape(256, 128),
            stT=colT(state_rnn[i, 0].reshape(-1)),
            cT=cT,
        )
        in_maps.append(m)
    if "nc" not in _NC_CACHE:
        _NC_CACHE["nc"] = build_nc()
    res = run_bass_kernel_spmd(_NC_CACHE["nc"], in_maps, core_ids=list(range(NCORES)))
    R = res.results
    _LAST["R"] = R
    y_prompt = np.concatenate([r["y_p"].reshape(4, SEQP, D) for r in R], axis=0).astype(np.float32)
    y_sample = np.stack([r["y_s"] for r in R], axis=0).astype(np.float32)
    new_k = np.concatenate([r["nk"].reshape(4, 1, SEQP, 2, 64) for r in R], axis=0).astype(np.float32)
    new_v = np.concatenate([r["nv"].reshape(4, 1, SEQP, 2, 64) for r in R], axis=0).astype(np.float32)
    new_rnn = np.concatenate([r["rn"].reshape(4, 1, 2, 512) for r in R], axis=0).astype(np.float32)
    return (y_prompt, y_sample, new_k, new_v, new_rnn)
```

```python
import numpy as np
from contextlib import ExitStack
import concourse.bass as bass
import concourse.mybir as mybir
from concourse.bass_utils import run_bass_kernel_spmd

F32 = mybir.dt.float32
BF16 = mybir.dt.bfloat16
I32 = mybir.dt.int32
U32 = mybir.dt.uint32
AF = mybir.ActivationFunctionType
ALU = mybir.AluOpType
AX = mybir.AxisListType

NCORES = 8
D = 1024
KC = 8
NT = 16
NTP = 8
SEQP = 256
SEQS = 1024
D_IN = 1792
ATTN_SCALE = 64 ** -0.5
EPS = 1e-6
NEXP = 16384
GELU_C = 0.7978845608028654

STAGE_B = True
STOP = None
DEBUG = False
_LAST = {}


class StopBuild(Exception):
    pass


_KBREF = {}


def _chk(name):
    if STOP == name:
        _KBREF["kb"].stop = True
NGB = 16
GS = 4


class Buf:
    def __init__(self, name):
        self.name = name
        self.w = {}
        self.r = {}


class TB(Buf):
    def __init__(self, name, t):
        super().__init__(name)
        self.t = t

    def __getitem__(self, k):
        return self.t[k]


class KB:
    def __init__(self, nc, es):
        self.nc = nc
        self.es = es
        self.es_sem = es
        self.stop = False
        _KBREF["kb"] = self
        self.engs = {"pe": nc.tensor, "act": nc.scalar, "dve": nc.vector, "pool": nc.gpsimd, "sp": nc.sync}
        self.sems = {}
        self.cnt = {}
        self.waited = {e: {} for e in self.engs}
        for e in self.engs:
            self.newsem(e)

    def newsem(self, name):
        self.sems[name] = self.es_sem.enter_context(self.nc.semaphore("s_" + name))
        self.cnt[name] = 0

    def sb(self, name, shape, dt):
        nbytes = int(np.prod(shape[1:])) * (2 if dt == BF16 else 4)
        self.acct = getattr(self, "acct", [])
        self.acct.append((id(self.es), name, nbytes))
        return TB(name, self.es.enter_context(self.nc.sbuf_tensor("sb_" + name, list(shape), dt)))

    def ps(self, name, shape, dt=F32):
        return TB(name, self.es.enter_context(self.nc.psum_tensor("pp_" + name, list(shape), dt)))

    def _wait(self, eng, sname, val):
        if eng == "pe" and sname == "pe":
            return
        if self.waited[eng].get(sname, 0) >= val:
            return
        self.engs[eng].wait_ge(self.sems[sname], val)
        self.waited[eng][sname] = val

    def _deps(self, eng, reads, writes, extra):
        for b in reads:
            for s, v in b.w.items():
                self._wait(eng, s, v)
        for b in writes:
            for s, v in b.w.items():
                self._wait(eng, s, v)
            for s, v in b.r.items():
                self._wait(eng, s, v)
        for (s, v) in extra:
            self._wait(eng, s, v)

    def _update(self, tok, reads, writes, join):
        s, v = tok
        for b in writes:
            if join:
                b.w[s] = max(b.w.get(s, 0), v)
            else:
                b.w = {s: v}
            b.r = {}
        for b in reads:
            if b in writes:
                continue
            b.r[s] = max(b.r.get(s, 0), v)

    def op(self, eng, fn, reads=(), writes=(), extra=()):
        if self.stop:
            return ("sp", 0)
        self._deps(eng, reads, writes, extra)
        inst = fn(self.engs[eng])
        self.cnt[eng] += 1
        inst.then_inc(self.sems[eng], 1)
        tok = (eng, self.cnt[eng])
        self._update(tok, reads, writes, False)
        return tok

    def dma(self, q, out, in_, sem, reads=(), writes=(), extra=(), join=False, **kw):
        if self.stop:
            return ("sp", 0)
        if sem not in self.sems:
            self.newsem(sem)
        if join:
            for b in reads:
                for s, v in b.w.items():
                    self._wait(q, s, v)
            for b in writes:
                for s, v in b.w.items():
                    if s != sem:
                        self._wait(q, s, v)
                for s, v in b.r.items():
                    self._wait(q, s, v)
            for (s, v) in extra:
                self._wait(q, s, v)
        else:
            self._deps(q, reads, writes, extra)
        inst = self.engs[q].dma_start(out=out, in_=in_, **kw)
        self.cnt[sem] += 16
        inst.then_inc(self.sems[sem], 16)
        tok = (sem, self.cnt[sem])
        self._update(tok, reads, writes, join)
        return tok

    def barrier(self, dma_sems=()):
        if self.stop:
            return
        snap = [(e, self.cnt[e]) for e in ("pe", "act", "dve", "pool", "sp")]
        snap += [(s_, self.cnt[s_]) for s_ in dma_sems if s_ in self.cnt]
        for e in self.engs:
            for (s_, v) in snap:
                if v > 0 and s_ != e:
                    self._wait(e, s_, v)

    def gather(self, out, table, idx_ap, sem, reads=(), writes=()):
        if self.stop:
            return ("sp", 0)
        if sem not in self.sems:
            self.newsem(sem)
        self._deps("pool", reads, writes, ())
        inst = self.nc.gpsimd.indirect_dma_start(
            out=out, out_offset=None, in_=table,
            in_offset=bass.IndirectOffsetOnAxis(ap=idx_ap, axis=0))
        self.cnt[sem] += 16
        inst.then_inc(self.sems[sem], 16)
        tok = (sem, self.cnt[sem])
        self._update(tok, reads, writes, False)
        return tok


def build_nc():
    nc = bass.Bass("TRN2", target_bir_lowering=False)

    def din(name, shape, dt=F32):
        return nc.dram_tensor(name, list(shape), dt, kind="ExternalInput").ap()

    def dout(name, shape, dt=F32):
        return nc.dram_tensor(name, list(shape), dt, kind="ExternalOutput").ap()

    xp_d = din("xp", [1024, D])
    xs_d = din("xs", [1024, D])
    ckT_d = din("ckT", [64, 2, 256])
    cv_d = din("cv", [256, 128])
    stT_d = din("stT", [128, 8])
    cT_d = din("cT", [128, 16])
    wmod_d = din("w_mod", [D, 6144])
    bmod_d = din("b_mod", [1, 6144])
    gmix_d = din("g_mix", [1, D])
    gffn_d = din("g_ffn", [1, D])
    gfin_d = din("g_fin", [1, D])
    win_d = din("w_in", [D, D_IN])
    wsw_d = din("w_sw", [D, 640])
    fv_d = din("fvec", [128, 48])
    rgw_d = din("rgw", [128, 16 * 128])
    sink_d = din("sink", [128, 4])
    wout_d = din("w_out", [D, D])
    wq_d = din("w_q", [D, 2048])
    skT_d = din("skT", [128, 16 * 128])
    pu_d = din("peer_u", [NEXP, D])
    pv_d = din("peer_v", [NEXP, D])
    ident_d = din("ident", [128, 128])
    cos_d = din("cosT", [64, 1024])
    sin_d = din("sinT", [64, 1024])
    mask_d = din("mask3", [128, 384])
    iota_d = din("iota16", [128, 16])
    onesv_d = din("onesv", [128, 256])

    yp_d = dout("y_p", [1024, D])
    ys_d = dout("y_s", [1024, D])
    nk_d = dout("nk", [1024, 128])
    nv_d = dout("nv", [1024, 128])
    rn_d = dout("rn", [32, 128])

    uv16 = nc.dram_tensor("uv16_scr", [NEXP, 2048], BF16, kind="Internal").ap()
    mods_d = nc.dram_tensor("mods_scr", [2, 6144], F32, kind="Internal").ap()
    x1_d = nc.dram_tensor("x1_scr", [2048, D], F32, kind="Internal").ap()

    es = ExitStack()
    with es:
        kb = KB(nc, es)
        op, dma = kb.op, kb.dma

        def dbg(name, tb, ap, shape, dt=F32):
            if DEBUG and not kb.stop:
                d = nc.dram_tensor("dbg_" + name, list(shape), dt, kind="ExternalOutput").ap()
                dma("sp", d, ap, "dbg", reads=[tb])

        psT = kb.ps("psT", [128, 1024])
        psP = [kb.ps(f"psP{i}", [128, 512]) for i in range(2)]
        psK = kb.ps("psK", [128, 512])
        psND = Buf("psND")
        psB = [kb.ps(f"psB{i}", [128, 512]) for i in range(2)]
        psC = kb.ps("psC", [128, 512])

        ident = kb.sb("ident", [128, 128], F32)
        cosT = kb.sb("cosT", [64, 1024], F32)
        sinT = kb.sb("sinT", [64, 1024], F32)
        mask3 = kb.sb("mask3", [128, 384], BF16)
        iota16 = kb.sb("iota16", [128, 16], F32)
        onesv = kb.sb("onesv", [128, 256], BF16)
        fvec = kb.sb("fvec", [128, 48], F32)
        sinkT = kb.sb("sinkT", [128, 4], F32)
        esink = kb.sb("esink", [128, 4], F32)
        stT = kb.sb("stT", [128, 8], F32)
        dma("sp", ident[:], ident_d, "c_ident", writes=[ident])
        dma("sp", cosT[:], cos_d, "c_cos", writes=[cosT])
        dma("sp", sinT[:], sin_d, "c_sin", writes=[sinT])
        dma("pool", mask3[:], mask_d, "c_mask", writes=[mask3])
        dma("sp", iota16[:], iota_d, "c_iota", writes=[iota16])
        dma("pool", onesv[:], onesv_d, "c_ones", writes=[onesv])
        dma("sp", fvec[:], fv_d, "c_fvec", writes=[fvec])
        dma("sp", sinkT[:], sink_d, "c_sink", writes=[sinkT])
        dma("sp", stT[:], stT_d, "c_st", writes=[stT])
        op("act", lambda e: e.activation(out=esink[:], in_=sinkT[:], func=AF.Exp), reads=[sinkT], writes=[esink])
        rgs = kb.sb("rgs", [128, 16], F32)
        rgtmp = kb.sb("rgtmp", [128, 8], F32)
        op("act", lambda e: e.activation(out=rgtmp[:], in_=fvec[:, 36:44], func=AF.Exp, scale=-1.0),
           reads=[fvec], writes=[rgtmp])
        op("act", lambda e: e.activation(out=rgtmp[:], in_=rgtmp[:], func=AF.Ln, bias=1.0, scale=1.0),
           reads=[rgtmp], writes=[rgtmp])
        op("dve", lambda e: e.tensor_scalar(out=rgs[:, 0:8], in0=rgtmp[:], scalar1=-8.0, scalar2=None, op0=ALU.mult),
           reads=[rgtmp], writes=[rgs])
        op("dve", lambda e: e.tensor_scalar(out=rgs[:, 8:16], in0=rgtmp[:], scalar1=-16.0, scalar2=None, op0=ALU.mult),
           reads=[rgtmp], writes=[rgs])

        Gb = kb.sb("Gb", [128, D], F32)
        SHb = kb.sb("SHb", [128, D], F32)
        GAb = kb.sb("GAb", [128, D], F32)
        gtmp = kb.sb("gtmp", [128, D], F32)

        with ExitStack() as es0:
            kb.es = es0
            cT = kb.sb("cT", [128, 16], F32)
            sgT = kb.sb("sgT", [128, 16], F32)
            sT = kb.sb("sT", [128, 16], F32)
            bmod2 = kb.sb("bmod2", [2, 6144], F32)
            wm = [kb.sb(f"wm{i}", [128, KC, 512], F32) for i in range(4)]
            rows = [kb.sb(f"rows{i}", [2, 512], F32) for i in range(2)]
            dma("sp", cT[:], cT_d, "cT", writes=[cT])
            dma("sp", bmod2[:], bmod_d.to_broadcast([2, 6144]), "bmod", writes=[bmod2])
            op("act", lambda e: e.activation(out=sgT[:], in_=cT[:], func=AF.Sigmoid), reads=[cT], writes=[sgT])
            op("dve", lambda e: e.tensor_tensor(out=sT[:], in0=cT[:], in1=sgT[:], op=ALU.mult),
               reads=[cT, sgT], writes=[sT])
            wmod_v = wmod_d.rearrange("(kc p) n -> p kc n", p=128)
            dbg("sT", sT, sT[:], [128, 16])
            dbg("bmod2", bmod2, bmod2[:, 0:512], [2, 512])
            def load_wm(nb):
                w = wm[nb % 4]
                for hf in range(2):
                    dma("sp", w[:, hf * 4:(hf + 1) * 4, :], wmod_v[:, hf * 4:(hf + 1) * 4, nb * 512:(nb + 1) * 512],
                        f"wm{nb % 4}", writes=[w], join=(hf == 1))

            for nb in range(3):
                load_wm(nb)
            for nb in range(12):
                if nb + 3 < 12:
                    load_wm(nb + 3)
                w = wm[nb % 4]
                pm = psP[nb % 2]
                for kc in range(KC):
                    op("pe", lambda e, kc=kc, w=w, pm=pm: e.matmul(
                        pm[0:2, :], lhsT=sT[:, 2 * kc:2 * kc + 2], rhs=w[:, kc, :], start=(kc == 0), stop=(kc == KC - 1)),
                       reads=[sT, w], writes=[pm])
                r = rows[nb % 2]
                op("dve", lambda e, r=r, pm=pm, nb=nb: e.tensor_tensor(
                    out=r[:], in0=pm[0:2, :], in1=bmod2[:, nb * 512:(nb + 1) * 512], op=ALU.add),
                   reads=[pm, bmod2], writes=[r])
                dma("sp", mods_d[:, nb * 512:(nb + 1) * 512], r[:], f"mods_out{nb % 2}", reads=[r])
                if nb == 0:
                    dbg("rows0", r, r[:], [2, 512])
                    dbg("wm0", w, w[:], [128, KC, 512])
            kb.barrier(dma_sems=[k_ for k_ in kb.cnt if k_ not in kb.engs])
            kb.es = es
        mods_done = [("mods_out0", kb.cnt["mods_out0"]), ("mods_out1", kb.cnt["mods_out1"])]
        if DEBUG:
            dm = nc.dram_tensor("dbg_modsd", [2, 6144], F32, kind="ExternalOutput").ap()
            dma("sp", dm, mods_d, "dbg", extra=mods_done)
        stopped = STOP == "mods"

        def load_mods(which, stage, parts=("pro", "ga")):
            base = 0 if stage == "A" else 3 * D
            gsrc = gmix_d if stage == "A" else gffn_d

            def row(off):
                return mods_d[which:which + 1, base + off:base + off + D].to_broadcast([128, D])
            if "ga" in parts:
                dma("sp", GAb[:], row(2 * D), "m_ga", writes=[GAb], extra=mods_done)
            if "pro" in parts:
                dma("sp", SHb[:], row(0), "m_sh", writes=[SHb], extra=mods_done)
                dma("sp", Gb[:], row(D), "m_g", writes=[Gb], extra=mods_done)
                dma("sp", gtmp[:], gsrc.to_broadcast([128, D]), "m_gt", writes=[gtmp])
                op("dve", lambda e: e.scalar_tensor_tensor(out=Gb[:], in0=Gb[:], scalar=1.0, in1=gtmp[:],
                                                           op0=ALU.add, op1=ALU.mult),
                   reads=[gtmp], writes=[Gb])

        uvbuf = Buf("uvbuf")
        conv_i = [0]

        def conv_chunks(n):
            for _ in range(n):
                i = conv_i[0]
                if i >= 32:
                    return
                conv_i[0] += 1
                tab = pu_d if i < 16 else pv_d
                col = 0 if i < 16 else 1024
                r = (i % 16) * 1024
                dma("pool", uv16[r:r + 1024, col:col + 1024], tab[r:r + 1024, :], "conv", writes=[uvbuf], join=True)

        junk_act = kb.sb("junk_act", [128, D], BF16)
        junk_dve = kb.sb("junk_dve", [128, D], BF16)

        def norm_stats(xt, name_i, stats):
            ss, rt, rstd = stats
            op("act", lambda e: e.activation(out=junk_act[:], in_=xt[:], func=AF.Square, accum_out=ss[:]),
               reads=[xt], writes=[ss])
            op("act", lambda e: e.activation(out=rt[:], in_=ss[:], func=AF.Sqrt, scale=1.0 / D, bias=EPS),
               reads=[ss], writes=[rt])
            op("dve", lambda e: e.reciprocal(out=rstd[:], in_=rt[:]), reads=[rt], writes=[rstd])
            return rstd

        try:
          if stopped:
            kb.stop = True
          with ExitStack() as esA:
              kb.es = esA
              w_in = kb.sb("w_in", [128, KC, D_IN], BF16)
              w_out = kb.sb("w_out", [128, KC, D], BF16)
              rgw = kb.sb("rgw", [128, 16, 128], F32)
              wsw = kb.sb("wsw", [128, KC, 640], BF16)
              wsw_v = wsw_d.rearrange("(kc p) n -> p kc n", p=128)
              for kc in range(0, KC, 4):
                  dma("pool", wsw[:, kc:kc + 4, :], wsw_v[:, kc:kc + 4, :], "wsw", writes=[wsw], join=(kc > 0))
              win_v = win_d.rearrange("(kc p) n -> p kc n", p=128)
              for kc in range(KC):
                  dma("pool", w_in[:, kc, :], win_v[:, kc, :], "w_in", writes=[w_in], join=(kc > 0))
              wout_v = wout_d.rearrange("(kc p) n -> p kc n", p=128)
              for kc in range(KC):
                  dma("pool", w_out[:, kc, :], wout_v[:, kc, :], "w_out", writes=[w_out], join=(kc > 0))
              dma("sp", rgw[:], rgw_d.rearrange("p (a b) -> p a b", a=16), "rgw", writes=[rgw])

              xt_ring = [kb.sb(f"xt{i}", [128, D], F32) for i in range(2)]
              stats_ring = [[kb.sb(f"st{i}_{j}", [128, 1], F32) for j in range(3)] for i in range(2)]
              hm = kb.sb("hm", [128, KC, SEQS], BF16)
              qT = kb.sb("qT", [64, 8, SEQS], BF16)
              kT = kb.sb("kT", [64, 2, SEQS], BF16)
              vz = kb.sb("vz", [128, 8, 2, 256], BF16)
              cvz = kb.sb("cvz", [128, 2, 2, 256], BF16)
              ckT = kb.sb("ckT", [64, 2, 256], BF16)
              kvf = [kb.sb(f"kvf{i}", [128, 256], F32) for i in range(2)]
              PT = [kb.sb(f"PT{i}", [128, 2, 640], BF16) for i in range(2)]
              rt1 = kb.sb("rt1", [64, 512], F32)
              rt2 = kb.sb("rt2", [64, 512], F32)
              xr = kb.sb("xr", [128, 4, SEQS + 3], F32)
              gy = kb.sb("gy", [128, 4, SEQS], BF16)
              gtm = [kb.sb(f"gtm{i}", [128, 512], F32) for i in range(3)]
              xc = kb.sb("xc", [128, SEQS], F32)
              rg_r = kb.sb("rg_r", [128, 512], F32)
              rg_t = kb.sb("rg_t", [128, 512], F32)
              rg_g = kb.sb("rg_g", [128, 512], F32)
              hdir = [kb.sb(f"hdir{i}", [128, SEQS], F32) for i in range(2)]
              rnn_o = kb.sb("rnn_o", [128, 32], F32)
              rnn_t = kb.sb("rnn_t", [32, 128], F32)
              den_r = kb.sb("den_r", [128, 128], F32)

              op("pool", lambda e: e.memset(vz[:], 0), writes=[vz])
              op("pool", lambda e: e.memset(cvz[:], 0), writes=[cvz])
              op("pool", lambda e: e.memset(xr[:], 0), writes=[xr])
              dma("pool", ckT[:], ckT_d, "ckT", writes=[ckT])
              cv_v = cv_d.rearrange("(kt p) (k d) -> p kt k d", p=128, k=2)
              for kvh in range(2):
                  for var in range(2):
                      dma("pool", cvz[:, :, kvh, var * 128 + var * 64: var * 128 + var * 64 + 64],
                          cv_v[:, :, kvh, :], "cvz", writes=[cvz], join=True)

              def mixer_segment(x_src, tok0, S, kind, seq_idx):
                  ntile = S // 128
                  is_s = (kind == "s")
                  for tl in range(ntile):
                      xt = xt_ring[tl % 2]
                      stt = stats_ring[tl % 2]
                      r0 = tok0 + tl * 128
                      dma("sp", xt[:], x_src[r0:r0 + 128, :], f"xt{tl % 2}", writes=[xt])
                      rstd = norm_stats(xt, tl, stt)
                      op("dve", lambda e, xt=xt, rstd=rstd: e.scalar_tensor_tensor(
                          out=xt[:], in0=xt[:], scalar=rstd[:, 0:1], in1=Gb[:], op0=ALU.mult, op1=ALU.mult),
                         reads=[rstd, Gb], writes=[xt])
                      op("dve", lambda e, xt=xt: e.tensor_tensor(out=xt[:], in0=xt[:], in1=SHb[:], op=ALU.add),
                         reads=[SHb], writes=[xt])
                      for kc in range(KC):
                          op("pe", lambda e, xt=xt, kc=kc: e.transpose(
                              psT[:, kc * 128:(kc + 1) * 128], xt[:, kc * 128:(kc + 1) * 128], ident[:]),
                             reads=[xt, ident], writes=[psT])
                      op("act", lambda e, tl=tl: e.activation(
                          out=hm[:, :, tl * 128:(tl + 1) * 128],
                          in_=psT[:].rearrange("p (k t) -> p k t", k=KC), func=AF.Copy),
                         reads=[psT], writes=[hm])
                  if seq_idx == 0 and not is_s:
                      dbg("Gb", Gb, Gb[:], [128, D])
                      dbg("SHb", SHb, SHb[:], [128, D])
                      dbg("GAb", GAb, GAb[:], [128, D])
                      dbg("xt1", xt_ring[1], xt_ring[1][:], [128, D])
                      dbg("rstd1", stats_ring[1][2], stats_ring[1][2][:], [128, 1])
                      dbg("hm", hm, hm[:, :, 0:256], [128, KC, 256], BF16)
                  _chk("norm")
                  NB = min(512, S)
                  pidx = [0]

                  def proj(cols_ap_fn, M, n0, wbuf=w_in):
                      pp = psP[pidx[0] % 2]
                      pidx[0] += 1
                      for kc in range(KC):
                          op("pe", lambda e, kc=kc, pp=pp: e.matmul(
                              pp[0:M, 0:NB], lhsT=cols_ap_fn(kc), rhs=hm[:, kc, n0:n0 + NB],
                              start=(kc == 0), stop=(kc == KC - 1)),
                             reads=[wbuf, hm], writes=[pp])
                      return pp

                  for nb in range(S // NB):
                      n0 = nb * NB
                      for g in range(10):
                          c0 = g * 64
                          dst = qT[:, g, n0:n0 + NB] if g < 8 else kT[:, g - 8, n0:n0 + NB]
                          dbuf = qT if g < 8 else kT
                          pp = proj(lambda kc, c0=c0: w_in[:, kc, c0:c0 + 64], 64, n0)
                          if not is_s:
                              op("act", lambda e, pp=pp, dst=dst: e.activation(out=dst, in_=pp[0:64, 0:NB], func=AF.Copy),
                                 reads=[pp], writes=[dbuf])
                          else:
                              pos0 = n0
                              pp2 = proj(lambda kc, c0=c0: wsw[:, kc, c0:c0 + 64], 64, n0, wbuf=wsw)
                              op("dve", lambda e, pp=pp: e.tensor_tensor(
                                  out=rt1[:, 0:NB], in0=pp[0:64, 0:NB], in1=cosT[:, pos0:pos0 + NB], op=ALU.mult),
                                 reads=[pp, cosT], writes=[rt1])
                              op("dve", lambda e, pp2=pp2: e.tensor_tensor(
                                  out=rt2[:, 0:NB], in0=pp2[0:64, 0:NB], in1=sinT[:, pos0:pos0 + NB], op=ALU.mult),
                                 reads=[pp2, sinT], writes=[rt2])
                              op("dve", lambda e, dst=dst: e.tensor_tensor(
                                  out=dst, in0=rt1[:, 0:NB], in1=rt2[:, 0:NB], op=ALU.add),
                                 reads=[rt1, rt2], writes=[dbuf])
                      _chk("projqk")
                      for ch in range(4):
                          c0 = 768 + ch * 128
                          pp = proj(lambda kc, c0=c0: w_in[:, kc, c0:c0 + 128], 128, n0)
                          op("act", lambda e, pp=pp, ch=ch: e.activation(
                              out=xr[:, ch, 2 + n0:2 + n0 + NB], in_=pp[:, 0:NB], func=AF.Copy),
                             reads=[pp], writes=[xr])
                      _chk("projxr")
                      for ch in range(4):
                          c0 = 1280 + ch * 128
                          pp = proj(lambda kc, c0=c0: w_in[:, kc, c0:c0 + 128], 128, n0)
                          g0, g1, g2 = gtm
                          op("act", lambda e, pp=pp: e.activation(out=g0[:, 0:NB], in_=pp[:, 0:NB], func=AF.Square),
                             reads=[pp], writes=[g0])
                          op("dve", lambda e: e.tensor_scalar(out=g0[:, 0:NB], in0=g0[:, 0:NB], scalar1=0.044715,
                                                              scalar2=1.0, op0=ALU.mult, op1=ALU.add),
                             reads=[], writes=[g0])
                          op("dve", lambda e, pp=pp: e.tensor_tensor(out=g1[:, 0:NB], in0=g0[:, 0:NB], in1=pp[:, 0:NB],
                                                                    op=ALU.mult),
                             reads=[g0, pp], writes=[g1])
                          op("act", lambda e: e.activation(out=g2[:, 0:NB], in_=g1[:, 0:NB], func=AF.Sigmoid,
                                                           scale=2.0 * GELU_C),
                             reads=[g1], writes=[g2])
                          op("dve", lambda e, pp=pp, ch=ch: e.tensor_tensor(
                              out=gy[:, ch, n0:n0 + NB], in0=g2[:, 0:NB], in1=pp[:, 0:NB], op=ALU.mult),
                             reads=[g2, pp], writes=[gy])
                      _chk("projgy")
                      for tl in range(NB // 128):
                          t_abs = n0 // 128 + tl
                          c_tok = n0 + tl * 128
                          for kc in range(KC):
                              op("pe", lambda e, kc=kc, c_tok=c_tok: e.matmul(
                                  psK[:, 0:256], lhsT=hm[:, kc, c_tok:c_tok + 128], rhs=w_in[:, kc, 512:768],
                                  start=(kc == 0), stop=(kc == KC - 1)),
                                 reads=[w_in, hm], writes=[psK])
                          _chk("kv1")
                          for var in range(2):
                              op("act", lambda e, var=var, t_abs=t_abs: e.activation(
                                  out=vz[:, t_abs, :, var * 192: var * 192 + 64],
                                  in_=psK[:, 128:256].rearrange("p (k d) -> p k d", k=2), func=AF.Copy),
                                 reads=[psK], writes=[vz])
                          _chk("kv2")
                          if not is_s:
                              kf = kvf[t_abs % 2]
                              op("act", lambda e, kf=kf: e.activation(out=kf[:], in_=psK[:, 0:256], func=AF.Copy),
                                 reads=[psK], writes=[kf])
                              _chk("kv3")
                              r0 = tok0 + t_abs * 128
                              dma("sp", nk_d[r0:r0 + 128, :], kf[:, 0:128], f"o_nk{t_abs % 2}", reads=[kf])
                              dma("sp", nv_d[r0:r0 + 128, :], kf[:, 128:256], f"o_nv{t_abs % 2}", reads=[kf])

                  _chk("proj")
                  def rg_section():
                      HB = min(512, S)
                      nh = S // HB
                      for ch in range(4):
                          yield op("dve", lambda e, ch=ch: e.tensor_scalar(
                              out=xc[:, 0:S], in0=xr[:, ch, 0:S], scalar1=fvec[:, ch * 4:ch * 4 + 1],
                              scalar2=fvec[:, 16 + ch:17 + ch], op0=ALU.mult, op1=ALU.add),
                             reads=[xr, fvec], writes=[xc])
                          for jt in range(1, 4):
                              yield op("dve", lambda e, ch=ch, jt=jt: e.scalar_tensor_tensor(
                                  out=xc[:, 0:S], in0=xr[:, ch, jt:jt + S], scalar=fvec[:, ch * 4 + jt:ch * 4 + jt + 1],
                                  in1=xc[:, 0:S], op0=ALU.mult, op1=ALU.add),
                                 reads=[xr, fvec], writes=[xc])
                          for dr in range(2):
                              hd = hdir[dr]
                              order = list(range(nh)) if dr == 0 else list(range(nh - 1, -1, -1))
                              for oi, hb_ in enumerate(order):
                                  c0 = hb_ * HB
                                  pa, pi_ = psP[0], psP[1]
                                  yield op("pe", lambda e, c0=c0, dr=dr, ch=ch: e.matmul(
                                      pa[:, 0:HB], lhsT=rgw[:, (0 * 2 + dr) * 4 + ch, :], rhs=xc[:, c0:c0 + HB],
                                      start=True, stop=True), reads=[rgw, xc], writes=[pa])
                                  yield op("pe", lambda e, c0=c0, dr=dr, ch=ch: e.matmul(
                                      pi_[:, 0:HB], lhsT=rgw[:, (1 * 2 + dr) * 4 + ch, :], rhs=xc[:, c0:c0 + HB],
                                      start=True, stop=True), reads=[rgw, xc], writes=[pi_])
                                  fi = dr * 4 + ch
                                  yield op("act", lambda e, fi=fi: e.activation(
                                      out=rg_r[:, 0:HB], in_=pa[:, 0:HB], func=AF.Sigmoid, bias=fvec[:, 20 + fi:21 + fi]),
                                     reads=[pa, fvec], writes=[rg_r])
                                  yield op("act", lambda e, fi=fi: e.activation(
                                      out=rg_g[:, 0:HB], in_=pi_[:, 0:HB], func=AF.Sigmoid, bias=fvec[:, 28 + fi:29 + fi]),
                                     reads=[pi_, fvec], writes=[rg_g])
                                  yield op("act", lambda e, fi=fi: e.activation(
                                      out=rg_t[:, 0:HB], in_=rg_r[:, 0:HB], func=AF.Exp, scale=rgs[:, 8 + fi:9 + fi]),
                                     reads=[rg_r, rgs], writes=[rg_t])
                                  yield op("act", lambda e, fi=fi: e.activation(
                                      out=rg_r[:, 0:HB], in_=rg_r[:, 0:HB], func=AF.Exp, scale=rgs[:, fi:fi + 1]),
                                     reads=[rgs], writes=[rg_r])
                                  yield op("dve", lambda e: e.tensor_scalar(
                                      out=rg_t[:, 0:HB], in0=rg_t[:, 0:HB], scalar1=1.0, scalar2=-1.0, op0=ALU.min,
                                      op1=ALU.mult), reads=[], writes=[rg_t])
                                  yield op("act", lambda e: e.activation(
                                      out=rg_t[:, 0:HB], in_=rg_t[:, 0:HB], func=AF.Sqrt, scale=1.0, bias=1.0),
                                     reads=[], writes=[rg_t])
                                  yield op("dve", lambda e: e.tensor_tensor(out=rg_g[:, 0:HB], in0=rg_g[:, 0:HB], in1=rg_t[:, 0:HB],
                                                                      op=ALU.mult), reads=[rg_t], writes=[rg_g])
                                  yield op("dve", lambda e, c0=c0: e.tensor_tensor(
                                      out=rg_g[:, 0:HB], in0=rg_g[:, 0:HB], in1=xc[:, c0:c0 + HB], op=ALU.mult),
                                     reads=[xc], writes=[rg_g])
                                  if oi == 0:
                                      init = stT[:, dr * 4 + ch:dr * 4 + ch + 1] if is_s else 0.0
                                      rd_init = [stT] if is_s else []
                                  else:
                                      if dr == 0:
                                          init = hd[:, c0 - 1:c0]
                                      else:
                                          init = hd[:, c0 + HB:c0 + HB + 1]
                                      rd_init = []
                                  if dr == 0:
                                      yield op("dve", lambda e, hd=hd, c0=c0, init=init: e.tensor_tensor_scan(
                                          out=hd[:, c0:c0 + HB], data0=rg_r[:, 0:HB], data1=rg_g[:, 0:HB], initial=init,
                                          op0=ALU.mult, op1=ALU.add),
                                         reads=[rg_r, rg_g] + rd_init, writes=[hd])
                                  else:
                                      yield op("dve", lambda e, hd=hd, c0=c0, init=init: e.tensor_tensor_scan(
                                          out=hd[:, c0:c0 + HB][:, ::-1], data0=rg_r[:, 0:HB][:, ::-1],
                                          data1=rg_g[:, 0:HB][:, ::-1], initial=init, op0=ALU.mult, op1=ALU.add),
                                         reads=[rg_r, rg_g] + rd_init, writes=[hd])
                          if not is_s:
                              c_f = (seq_idx * 2 + 0) * 4 + ch
                              c_b = (seq_idx * 2 + 1) * 4 + ch
                              yield op("act", lambda e, c_f=c_f: e.activation(out=rnn_o[:, c_f:c_f + 1], in_=hdir[0][:, S - 1:S],
                                                                        func=AF.Copy), reads=[hdir[0]], writes=[rnn_o])
                              yield op("act", lambda e, c_b=c_b: e.activation(out=rnn_o[:, c_b:c_b + 1], in_=hdir[1][:, 0:1],
                                                                        func=AF.Copy), reads=[hdir[1]], writes=[rnn_o])
                          yield op("dve", lambda e: e.tensor_tensor(out=hdir[0][:, 0:S], in0=hdir[0][:, 0:S], in1=hdir[1][:, 0:S],
                                                              op=ALU.add), reads=[hdir[1]], writes=[hdir[0]])
                          yield op("dve", lambda e, ch=ch: e.tensor_tensor(out=hm[:, 4 + ch, 0:S], in0=hdir[0][:, 0:S],
                                                                    in1=gy[:, ch, 0:S], op=ALU.mult),
                             reads=[hdir[0], gy], writes=[hm])


                  rgen = rg_section()

                  def rg_pull(n):
                      for _ in range(n):
                          next(rgen, None)

                  nblk = S // 128
                  pti = [0]
                  for b in range(nblk):
                      if is_s:
                          slots = [kt for kt in (b - 1, b, b + 1)]
                      else:
                          slots = list(range(nblk))
                      for j in range(4):
                          kvh = j // 2
                          P = PT[pti[0] % 2]
                          pti[0] += 1
                          rg_pull(6)
                          for hh in range(2):
                              h = 2 * j + hh
                              pb = psB[hh]
                              lo, hi = None, None
                              for si, kt in enumerate(slots):
                                  if kt < 0 or kt >= nblk:
                                      continue
                                  if lo is None:
                                      lo = si
                                  hi = si + 1
                                  op("pe", lambda e, pb=pb, si=si, kt=kt, h=h: e.matmul(
                                      pb[:, si * 128:(si + 1) * 128], lhsT=kT[:, kvh, kt * 128:(kt + 1) * 128],
                                      rhs=qT[:, h, b * 128:(b + 1) * 128], start=True, stop=True),
                                     reads=[kT, qT], writes=[pb])
                              op("act", lambda e, pb=pb, hh=hh, P=P, lo=lo, hi=hi: e.activation(
                                  out=P[:, hh, lo * 128:hi * 128], in_=pb[:, lo * 128:hi * 128], func=AF.Exp,
                                  scale=ATTN_SCALE),
                                 reads=[pb], writes=[P])
                              if is_s:
                                  for ct in range(2):
                                      op("pe", lambda e, ct=ct, hh=hh, h=h: e.matmul(
                                          psC[:, (hh * 2 + ct) * 128:(hh * 2 + ct + 1) * 128],
                                          lhsT=ckT[:, kvh, ct * 128:(ct + 1) * 128],
                                          rhs=qT[:, h, b * 128:(b + 1) * 128], start=True, stop=True),
                                         reads=[ckT, qT], writes=[psC])
                          if is_s:
                              op("act", lambda e, P=P: e.activation(
                                  out=P[:, :, 384:640], in_=psC[:].rearrange("p (h c) -> p h c", h=2), func=AF.Exp,
                                  scale=ATTN_SCALE),
                                 reads=[psC], writes=[P])
                              lo = 1 if b == 0 else 0
                              hi = 2 if b == nblk - 1 else 3
                              op("dve", lambda e, P=P, lo=lo, hi=hi: e.tensor_tensor(
                                  out=P[:, :, lo * 128:hi * 128], in0=P[:, :, lo * 128:hi * 128],
                                  in1=mask3[:, lo * 128:hi * 128].unsqueeze(1).to_broadcast([128, 2, (hi - lo) * 128]),
                                  op=ALU.mult),
                                 reads=[mask3], writes=[P])
                          terms = []
                          rg_pull(6)
                          for hh in range(2):
                              for si, kt in enumerate(slots):
                                  if kt < 0 or kt >= nblk:
                                      continue
                                  terms.append((hh, si, vz, kt))
                              if is_s:
                                  for ct in range(2):
                                      terms.append((hh, 3 + ct, cvz, ct))
                          rg_pull(4)
                          for which in range(2):
                              for ti, (hh, si, vsrc, kt) in enumerate(terms):
                                  if which == 0:
                                      lhs = vsrc[:, kt, kvh, hh * 128:(hh + 1) * 128]
                                      rd = [vsrc, P]
                                  else:
                                      lhs = onesv[:, hh * 128:(hh + 1) * 128]
                                      rd = [onesv, P]
                                  op("pe", lambda e, lhs=lhs, hh=hh, si=si, which=which, ti=ti, P=P: e.matmul(
                                      psK[:, 256 + which * 128:256 + (which + 1) * 128], lhsT=lhs,
                                      rhs=P[:, hh, si * 128:(si + 1) * 128],
                                      start=(ti == 0), stop=(ti == len(terms) - 1)),
                                     reads=rd, writes=[psND])
                          op("dve", lambda e, j=j: e.tensor_scalar(
                              out=den_r[:], in0=psK[:, 384:512], scalar1=esink[:, j:j + 1], scalar2=None, op0=ALU.add),
                             reads=[psND, esink], writes=[den_r])
                          op("dve", lambda e: e.reciprocal(out=den_r[:], in_=den_r[:]), reads=[], writes=[den_r])
                          op("dve", lambda e, j=j, b=b: e.tensor_tensor(
                              out=hm[:, j, b * 128:(b + 1) * 128], in0=psK[:, 256:384], in1=den_r[:], op=ALU.mult),
                             reads=[psND, den_r], writes=[hm])

                  for _ in rgen:
                      pass
                  _chk("attn")
                  if is_s:
                      dbg("mixs", hm, hm[:], [128, KC, SEQS], BF16)
                  elif seq_idx == 0:
                      dbg("mixp0", hm, hm[:, :, 0:256], [128, KC, 256], BF16)
                  _chk("rg")
                  for tl in range(ntile):
                      xt = xt_ring[tl % 2]
                      r0 = tok0 + tl * 128
                      dma("sp", xt[:], x_src[r0:r0 + 128, :], f"xt{tl % 2}", writes=[xt])
                      for nbk in range(2):
                          for mc in range(KC):
                              op("pe", lambda e, nbk=nbk, mc=mc, tl=tl: e.matmul(
                                  psT[:, nbk * 512:(nbk + 1) * 512], lhsT=hm[:, mc, tl * 128:(tl + 1) * 128],
                                  rhs=w_out[:, mc, nbk * 512:(nbk + 1) * 512], start=(mc == 0), stop=(mc == KC - 1)),
                                 reads=[hm, w_out], writes=[psT])
                      op("dve", lambda e: e.tensor_tensor(out=gtmp[:], in0=psT[:], in1=GAb[:], op=ALU.mult),
                         reads=[psT, GAb], writes=[gtmp])
                      op("dve", lambda e, xt=xt: e.tensor_tensor(out=xt[:], in0=xt[:], in1=gtmp[:], op=ALU.add),
                         reads=[gtmp], writes=[xt])
                      g0 = (1024 if is_s else 0) + r0
                      dma("sp", x1_d[g0:g0 + 128, :], xt[:], f"x1_out{tl % 2}", reads=[xt])

              load_mods(0, "A")
              for s in range(4):
                  mixer_segment(xp_d, s * SEQP, SEQP, "p", s)
                  if s == 0:
                      conv_chunks(32)
              op("pe", lambda e: e.transpose(psT[0:32, 0:128], rnn_o[:, 0:32], ident[:]),
                 reads=[rnn_o, ident], writes=[psT])
              op("act", lambda e: e.activation(out=rnn_t[:], in_=psT[0:32, 0:128], func=AF.Copy),
                 reads=[psT], writes=[rnn_t])
              dma("sp", rn_d, rnn_t[:], "o_rn", reads=[rnn_t])
              load_mods(1, "A")
              mixer_segment(xs_d, 0, SEQS, "s", 0)
              kb.es = es
        except StopBuild:
            stopped = True
            kb.es = es
        x1_done = [("x1_out0", kb.cnt.get("x1_out0", 0)), ("x1_out1", kb.cnt.get("x1_out1", 0))]
        if DEBUG and not kb.stop:
            dx = nc.dram_tensor("dbg_x1", [2048, D], F32, kind="ExternalOutput").ap()
            dma("sp", dx, x1_d, "dbg", extra=x1_done)

        if STAGE_B and not kb.stop:
            with ExitStack() as esB:
                kb.es = esB
                kb.barrier(dma_sems=[k_ for k_ in kb.cnt if k_ not in kb.engs])
                wq = kb.sb("wq", [128, KC, 2048], BF16)
                skT = kb.sb("skT", [128, 16, 128], BF16)
                GFb = kb.sb("GFb", [128, D], F32)
                wq_v = wq_d.rearrange("(kc p) n -> p kc n", p=128)
                for kc in range(KC):
                    dma("pool", wq[:, kc, :], wq_v[:, kc, :], "wq", writes=[wq], join=(kc > 0))
                dma("pool", skT[:], skT_d.rearrange("p (a b) -> p a b", a=16), "skT", writes=[skT])
                dma("sp", GFb[:], gfin_d.to_broadcast([128, D]), "gfb", writes=[GFb])

                x1t = [kb.sb(f"x1t{i}", [128, D], F32) for i in range(2)]
                h2 = [kb.sb(f"h2_{i}", [128, D], F32) for i in range(2)]
                h2b = [kb.sb(f"h2b_{i}", [128, D], BF16) for i in range(2)]
                stB = [[kb.sb(f"stB{i}_{j}", [128, 1], F32) for j in range(3)] for i in range(2)]
                h2T = kb.sb("h2T", [128, KC, 128], BF16)
                qpT = kb.sb("qpT", [128, 16, 128], BF16)
                ssb = kb.sb("ssb", [128, 16, 128], F32)
                v16 = kb.sb("v16", [128, 16, 16], F32)
                i16 = kb.sb("i16", [128, 16, 16], U32)
                i16f = kb.sb("i16f", [128, 16, 16], F32)
                cand = kb.sb("cand", [128, 8, 256], F32)
                best = kb.sb("best", [128, 8, 16], F32)
                pos = kb.sb("pos", [128, 8, 16], U32)
                pif = kb.sb("pif", [128, 8, 16], F32)
                pjf = kb.sb("pjf", [128, 8, 16], F32)
                pit = kb.sb("pit", [128, 8, 16], U32)
                eq = kb.sb("eq", [128, 8, 256], F32)
                cand2 = eq
                isel = kb.sb("isel", [128, 8, 16], F32)
                jsel = kb.sb("jsel", [128, 8, 16], F32)
                eidf = kb.sb("eidf", [128, 128], F32)
                eid = [kb.sb(f"eid{i}", [128, 128], I32) for i in range(2)]
                gwb = [kb.sb(f"gw{i}", [128, 8, 16], F32) for i in range(2)]
                gsum = kb.sb("gsum", [128, 8], F32)
                conv_chunks(32)
                gb = [kb.sb(f"gb{i}", [128, 2 * D], BF16) for i in range(NGB)]
                dg = [kb.sb(f"dg{i}", [128, 128], BF16) for i in range(4)]
                NGRP = 128 // GS
                av_ = [kb.sb(f"av{i}", [128, GS], F32) for i in range(2)]
                sq_ = [kb.sb(f"sq{i}", [128, GS], F32) for i in range(2)]
                zz_ = [kb.sb(f"zz{i}", [128, GS], F32) for i in range(2)]
                xg_ = [kb.sb(f"xg{i}", [128, GS], F32) for i in range(2)]
                sg_ = [kb.sb(f"sg{i}", [128, GS], F32) for i in range(2)]
                wg_ = [kb.sb(f"wg{i}", [128, GS], F32) for i in range(2)]
                di_ = [0]
                yt = kb.sb("yt", [128, D], F32)
                stF = [kb.sb(f"stF{j}", [128, 1], F32) for j in range(3)]
                gi_ = [0]

                def prologue(t):
                    is_s = t >= NTP
                    xb = x1t[t % 2]
                    hb = h2[t % 2]
                    ei = eid[t % 2]
                    gw = gwb[t % 2]
                    dma("sp", xb[:], x1_d[t * 128:(t + 1) * 128, :], f"x1t{t % 2}", writes=[xb], extra=x1_done)
                    rstd = norm_stats(xb, t, stB[t % 2])
                    yield op("dve", lambda e: e.scalar_tensor_tensor(
                        out=hb[:], in0=xb[:], scalar=rstd[:, 0:1], in1=Gb[:], op0=ALU.mult, op1=ALU.mult),
                       reads=[xb, rstd, Gb], writes=[hb])
                    yield op("dve", lambda e: e.tensor_tensor(out=hb[:], in0=hb[:], in1=SHb[:], op=ALU.add),
                       reads=[SHb], writes=[hb])
                    op("act", lambda e: e.activation(out=h2b[t % 2][:], in_=hb[:], func=AF.Copy),
                       reads=[hb], writes=[h2b[t % 2]])
                    for kc in range(KC):
                        op("pe", lambda e, kc=kc: e.transpose(
                            psT[:, kc * 128:(kc + 1) * 128], hb[:, kc * 128:(kc + 1) * 128], ident[:]),
                           reads=[hb, ident], writes=[psT])
                    op("act", lambda e: e.activation(out=h2T[:], in_=psT[:].rearrange("p (k t) -> p k t", k=KC),
                                                     func=AF.Copy), reads=[psT], writes=[h2T])
                    qbanks = [psP[0], psP[1], psP[0], psP[1]]
                    for c in range(16):
                        pq = qbanks[c // 4]
                        for kc in range(KC):
                            op("pe", lambda e, c=c, kc=kc, pq=pq: e.matmul(
                                pq[:, (c % 4) * 128:(c % 4 + 1) * 128], lhsT=wq[:, kc, c * 128:(c + 1) * 128],
                                rhs=h2T[:, kc, :], start=(kc == 0), stop=(kc == KC - 1)),
                               reads=[wq, h2T], writes=[pq])
                        if c % 4 == 3:
                            op("act", lambda e, c=c, pq=pq: e.activation(
                                out=qpT[:, c - 3:c + 1, :], in_=pq[:].rearrange("p (a b) -> p a b", a=4), func=AF.Copy),
                               reads=[pq], writes=[qpT])
                    sbanks = [psT, psT, psP[0], psP[1]]
                    for c in range(16):
                        bk = sbanks[c // 4]
                        off = (c // 4) * 512 if c // 4 < 2 else 0
                        op("pe", lambda e, c=c, bk=bk, off=off: e.matmul(
                            bk[:, off + (c % 4) * 128:off + (c % 4 + 1) * 128], lhsT=qpT[:, c, :], rhs=skT[:, c, :],
                            start=True, stop=True), reads=[qpT, skT], writes=[bk])
                    op("act", lambda e: e.activation(out=ssb[:, 0:8, :], in_=psT[:].rearrange("p (a b) -> p a b", a=8),
                                                     func=AF.Copy), reads=[psT], writes=[ssb])
                    op("act", lambda e: e.activation(out=ssb[:, 8:12, :], in_=psP[0][:].rearrange("p (a b) -> p a b", a=4),
                                                     func=AF.Copy), reads=[psP[0]], writes=[ssb])
                    op("act", lambda e: e.activation(out=ssb[:, 12:16, :], in_=psP[1][:].rearrange("p (a b) -> p a b", a=4),
                                                     func=AF.Copy), reads=[psP[1]], writes=[ssb])
                    for _ in range(24):
                        yield None
                    for c in range(16):
                        yield op("dve", lambda e, c=c: e.max(out=v16[:, c, 0:8], in_=ssb[:, c, :]), reads=[ssb], writes=[v16])
                        yield op("dve", lambda e, c=c: e.max_index(out=i16[:, c, 0:8], in_max=v16[:, c, 0:8],
                                                             in_values=ssb[:, c, :]), reads=[v16, ssb], writes=[i16])
                        yield op("dve", lambda e, c=c: e.match_replace(out=ssb[:, c, :], in_to_replace=v16[:, c, 0:8],
                                                                 in_values=ssb[:, c, :], imm_value=-1e30),
                           reads=[v16], writes=[ssb])
                        yield op("dve", lambda e, c=c: e.max(out=v16[:, c, 8:16], in_=ssb[:, c, :]), reads=[ssb], writes=[v16])
                        yield op("dve", lambda e, c=c: e.max_index(out=i16[:, c, 8:16], in_max=v16[:, c, 8:16],
                                                             in_values=ssb[:, c, :]), reads=[v16, ssb], writes=[i16])
                    yield op("dve", lambda e: e.tensor_copy(out=i16f[:], in_=i16[:]), reads=[i16], writes=[i16f])
                    v16v = v16[:].rearrange("p (h s) k -> p h s k", s=2)
                    yield op("dve", lambda e: e.tensor_tensor(
                        out=cand[:].rearrange("p h (i j) -> p h i j", i=16),
                        in0=v16v[:, :, 0, :].unsqueeze(3).to_broadcast([128, 8, 16, 16]),
                        in1=v16v[:, :, 1, :].unsqueeze(2).to_broadcast([128, 8, 16, 16]), op=ALU.add),
                       reads=[v16], writes=[cand])
                    for h in range(8):
                        yield op("dve", lambda e, h=h: e.max(out=best[:, h, 0:8], in_=cand[:, h, :]), reads=[cand], writes=[best])
                        yield op("dve", lambda e, h=h: e.max_index(out=pos[:, h, 0:8], in_max=best[:, h, 0:8],
                                                             in_values=cand[:, h, :]), reads=[best, cand], writes=[pos])
                        yield op("dve", lambda e, h=h: e.match_replace(out=cand2[:, h, :], in_to_replace=best[:, h, 0:8],
                                                                 in_values=cand[:, h, :], imm_value=-1e30),
                           reads=[best, cand], writes=[cand2])
                        yield op("dve", lambda e, h=h: e.max(out=best[:, h, 8:16], in_=cand2[:, h, :]),
                           reads=[cand2], writes=[best])
                        yield op("dve", lambda e, h=h: e.max_index(out=pos[:, h, 8:16], in_max=best[:, h, 8:16],
                                                             in_values=cand2[:, h, :]), reads=[best, cand2], writes=[pos])
                    yield op("dve", lambda e: e.tensor_single_scalar(out=pit[:], in_=pos[:], scalar=4,
                                                               op=ALU.logical_shift_right), reads=[pos], writes=[pit])
                    yield op("dve", lambda e: e.tensor_copy(out=pif[:], in_=pit[:]), reads=[pit], writes=[pif])
                    yield op("dve", lambda e: e.tensor_single_scalar(out=pit[:], in_=pos[:], scalar=15,
                                                               op=ALU.bitwise_and), reads=[pos], writes=[pit])
                    yield op("dve", lambda e: e.tensor_copy(out=pjf[:], in_=pit[:]), reads=[pit], writes=[pjf])
                    i16v = i16f[:].rearrange("p (h s) k -> p h s k", s=2)
                    iob = iota16[:].unsqueeze(1).unsqueeze(1).to_broadcast([128, 8, 16, 16])
                    for (pf, half, dst) in ((pif, 0, isel), (pjf, 1, jsel)):
                        yield op("dve", lambda e, pf=pf: e.tensor_tensor(
                            out=eq[:].rearrange("p h (i j) -> p h i j", i=16), in0=pf[:].unsqueeze(3).to_broadcast([128, 8, 16, 16]), in1=iob, op=ALU.is_equal),
                           reads=[pf, iota16], writes=[eq])
                        yield op("dve", lambda e, half=half: e.tensor_tensor(
                            out=eq[:].rearrange("p h (i j) -> p h i j", i=16), in0=eq[:].rearrange("p h (i j) -> p h i j", i=16), in1=i16v[:, :, half, :].unsqueeze(2).to_broadcast([128, 8, 16, 16]),
                            op=ALU.mult), reads=[i16f], writes=[eq])
                        yield op("dve", lambda e, dst=dst: e.tensor_reduce(out=dst[:], in_=eq[:].rearrange("p h (i j) -> p h i j", i=16), axis=AX.X, op=ALU.add),
                           reads=[eq], writes=[dst])
                    yield op("dve", lambda e: e.scalar_tensor_tensor(
                        out=eidf[:], in0=isel[:].rearrange("p h k -> p (h k)"), scalar=128.0,
                        in1=jsel[:].rearrange("p h k -> p (h k)"), op0=ALU.mult, op1=ALU.add),
                       reads=[isel, jsel], writes=[eidf])
                    yield op("dve", lambda e: e.tensor_copy(out=ei[:], in_=eidf[:]), reads=[eidf], writes=[ei])
                    yield op("dve", lambda e: e.tensor_tensor(
                        out=gw[:], in0=best[:], in1=best[:, :, 0:1].to_broadcast([128, 8, 16]), op=ALU.subtract),
                       reads=[best], writes=[gw])
                    op("act", lambda e: e.activation(out=gw[:], in_=gw[:], func=AF.Exp), reads=[], writes=[gw])
                    yield op("dve", lambda e: e.tensor_reduce(out=gsum[:], in_=gw[:], axis=AX.X, op=ALU.add),
                       reads=[gw], writes=[gsum])
                    yield op("dve", lambda e: e.reciprocal(out=gsum[:], in_=gsum[:]), reads=[], writes=[gsum])
                    yield op("dve", lambda e: e.tensor_tensor(
                        out=gw[:], in0=gw[:], in1=gsum[:].unsqueeze(2).to_broadcast([128, 8, 16]), op=ALU.mult),
                       reads=[gsum], writes=[gw])
                def body(t, gen):
                    is_s = t >= NTP
                    xb = x1t[t % 2]
                    hb = h2[t % 2]
                    ei = eid[t % 2]
                    gw = gwb[t % 2]
                    accb = (psB[0], psB[1]) if t % 2 == 0 else (psK, psC)

                    def pull():
                        if gen is not None:
                            next(gen, None)
                    gwf = gw[:].rearrange("p h k -> p (h k)")
                    slots = {}
                    pending = []

                    def finish_group(grp):
                        par = grp % 2
                        av, xg, sg, wg = av_[par], xg_[par], sg_[par], wg_[par]
                        op("dve", lambda e: e.tensor_tensor(out=wg[:], in0=sg[:], in1=xg[:], op=ALU.mult),
                           reads=[sg, xg], writes=[wg])
                        for k in range(GS):
                            c = grp * GS + k
                            dgb = dg[di_[0] % 4]
                            di_[0] += 1
                            op("act", lambda e: e.activation(out=dgb[:], in_=ident[:], func=AF.Copy, scale=wg[:, k:k + 1]),
                               reads=[wg, ident], writes=[dgb])
                            for half in range(2):
                                op("pe", lambda e: e.matmul(
                                    accb[half][:, 0:512], lhsT=dgb[:], rhs=slots[c][:, D + half * 512:D + (half + 1) * 512],
                                    start=(c == 0), stop=(c == 127)), reads=[dgb, slots[c]], writes=[accb[half]])

                    for grp in range(NGRP):
                        par = grp % 2
                        av, sq, zz, xg, sg = av_[par], sq_[par], zz_[par], xg_[par], sg_[par]
                        for k in range(GS):
                            c = grp * GS + k
                            g = gb[gi_[0] % NGB]
                            sname = f"gb{gi_[0] % NGB}"
                            gi_[0] += 1
                            slots[c] = g
                            kb.gather(g[:], uv16, ei[:, c:c + 1], sname, reads=[ei, uvbuf], writes=[g])
                            op("dve", lambda e: e.scalar_tensor_tensor(
                                out=junk_dve[:], in0=g[:, 0:D], scalar=1.0, in1=h2b[t % 2][:], op0=ALU.mult, op1=ALU.mult,
                                accum_out=av[:, k:k + 1]), reads=[g, h2b[t % 2]], writes=[av])
                            pull()
                        cs = slice(grp * GS, (grp + 1) * GS)
                        op("dve", lambda e: e.tensor_tensor(out=sq[:], in0=av[:], in1=av[:], op=ALU.mult),
                           reads=[av], writes=[sq])
                        op("dve", lambda e: e.tensor_tensor(out=sq[:], in0=sq[:], in1=av[:], op=ALU.mult),
                           reads=[av], writes=[sq])
                        op("dve", lambda e: e.scalar_tensor_tensor(out=zz[:], in0=sq[:], scalar=0.044715, in1=av[:],
                                                                   op0=ALU.mult, op1=ALU.add),
                           reads=[sq, av], writes=[zz])
                        op("act", lambda e: e.activation(out=sg[:], in_=zz[:], func=AF.Sigmoid, scale=2.0 * GELU_C),
                           reads=[zz], writes=[sg])
                        op("dve", lambda e: e.tensor_tensor(out=xg[:], in0=av[:], in1=gwf[:, cs], op=ALU.mult),
                           reads=[av, gw], writes=[xg])
                        if pending:
                            finish_group(pending.pop(0))
                        pending.append(grp)
                    while pending:
                        finish_group(pending.pop(0))
                    if gen is not None:
                        for _ in gen:
                            pass
                    for half in range(2):
                        hs = slice(half * 512, (half + 1) * 512)
                        op("dve", lambda e: e.tensor_tensor(out=yt[:, hs], in0=accb[half][:, 0:512], in1=GAb[:, hs], op=ALU.mult),
                           reads=[accb[half], GAb], writes=[yt])
                    op("dve", lambda e: e.tensor_tensor(out=yt[:], in0=yt[:], in1=xb[:], op=ALU.add),
                       reads=[xb], writes=[yt])
                    rstd3 = norm_stats(yt, t, stF)
                    op("dve", lambda e: e.scalar_tensor_tensor(
                        out=yt[:], in0=yt[:], scalar=rstd3[:, 0:1], in1=GFb[:], op0=ALU.mult, op1=ALU.mult),
                       reads=[rstd3, GFb], writes=[yt])
                    if is_s:
                        dst = ys_d[(t - NTP) * 128:(t - NTP + 1) * 128, :]
                    else:
                        dst = yp_d[t * 128:(t + 1) * 128, :]
                    dma("sp", dst, yt[:], "o_y", reads=[yt])

                load_mods(0, "B")
                for _ in prologue(0):
                    pass
                for t in range(NT):
                    nxt = None
                    if t + 1 < NT:
                        if t + 1 == NTP:
                            load_mods(1, "B", parts=("pro",))
                        nxt = prologue(t + 1)
                    body(t, nxt)
                    if t + 1 == NTP:
                        load_mods(1, "B", parts=("ga",))
                kb.es = es

        for s_ in kb.cnt:
            if kb.cnt[s_] > 0 and s_ != "sp":
                nc.sync.wait_ge(kb.sems[s_], kb.cnt[s_])
    return nc


def _consts():
    ident = np.eye(128, dtype=np.float32)
    pos = np.arange(1024)
    row = (pos // 64).astype(np.float32)
    col = (pos % 64).astype(np.float32)
    inv = (np.float32(10000.0) ** (-np.arange(16, dtype=np.float32) / np.float32(16))).astype(np.float32)
    ang = np.concatenate([row[:, None] * inv, col[:, None] * inv], axis=-1).astype(np.float32)
    c = np.cos(ang).astype(np.float32).T
    s = np.sin(ang).astype(np.float32).T
    cosT = np.concatenate([c, c], axis=0)
    sinT = np.concatenate([-s, s], axis=0)
    ki = np.arange(128)[:, None]
    qi = np.arange(128)[None, :]
    L = (qi <= ki).astype(np.float32)
    U = (ki <= qi).astype(np.float32)
    mask3 = np.concatenate([L, np.ones((128, 128), np.float32), U], axis=1)
    iota16 = np.tile(np.arange(16, dtype=np.float32)[None, :], (128, 1))
    onesv = np.zeros((128, 256), np.float32)
    onesv[:, 0:64] = 1.0
    onesv[:, 192:256] = 1.0
    return dict(ident=ident, cosT=np.ascontiguousarray(cosT), sinT=np.ascontiguousarray(sinT), mask3=mask3,
                iota16=iota16, onesv=onesv)


_NC_CACHE = {}


def kernel(x_prompt, x_sample, cache_k, cache_v, state_rnn, c, c_ctx, w_mod, b_mod,
           g_norm_mix, g_norm_ffn, w_in, conv_w, conv_b, rg_w_a, rg_b_a, rg_w_i, rg_b_i,
           rg_lambda, attn_sink, w_out, peer_w_query, peer_sub_keys, peer_u, peer_v, g_final):
    f = lambda a: np.ascontiguousarray(np.asarray(a, dtype=np.float32))
    x_prompt, x_sample = f(x_prompt), f(x_sample)
    consts = _consts()

    def colT(v):
        v = f(v).reshape(-1, 128)
        return np.ascontiguousarray(v.T)

    fvec = np.zeros((128, 48), np.float32)
    cw = f(conv_w)[0]
    for ch in range(4):
        for jt in range(4):
            fvec[:, ch * 4 + jt] = cw[jt, ch * 128:(ch + 1) * 128]
    fvec[:, 16:20] = colT(f(conv_b)[0])
    fvec[:, 20:28] = colT(f(rg_b_a)[0].reshape(-1))
    fvec[:, 28:36] = colT(f(rg_b_i)[0].reshape(-1))
    fvec[:, 36:44] = colT(f(rg_lambda)[0].reshape(-1))
    rgw = np.zeros((128, 16, 128), np.float32)
    for gate, W in enumerate((f(rg_w_a)[0], f(rg_w_i)[0])):
        for dr in range(2):
            for ch in range(4):
                for bb in range(2):
                    rgw[bb * 64:(bb + 1) * 64, (gate * 2 + dr) * 4 + ch, bb * 64:(bb + 1) * 64] = W[dr, ch * 2 + bb]
    sink = f(attn_sink)[0]
    sinkT = np.zeros((128, 4), np.float32)
    for j in range(4):
        sinkT[0:64, j] = sink[2 * j]
        sinkT[64:128, j] = sink[2 * j + 1]
    skT = np.ascontiguousarray(f(peer_sub_keys)[0].reshape(16, 128, 128).transpose(2, 0, 1)).reshape(128, 16 * 128)
    perm = np.concatenate([np.concatenate([np.arange(h * 64 + 32, h * 64 + 64), np.arange(h * 64, h * 64 + 32)])
                           for h in range(10)])
    w_sw = np.ascontiguousarray(f(w_in)[0][:, perm])
    shared = dict(
        w_sw=w_sw, w_mod=f(w_mod)[0], b_mod=f(b_mod)[0].reshape(1, -1), g_mix=f(g_norm_mix)[0].reshape(1, -1),
        g_ffn=f(g_norm_ffn)[0].reshape(1, -1), g_fin=f(g_final).reshape(1, -1), w_in=f(w_in)[0], fvec=fvec,
        rgw=rgw.reshape(128, -1), sink=sinkT, w_out=f(w_out)[0], w_q=f(peer_w_query)[0].reshape(D, 2048),
        skT=skT, peer_u=f(peer_u)[0], peer_v=f(peer_v)[0], **consts)
    cache_k, cache_v, state_rnn, c, c_ctx = f(cache_k), f(cache_v), f(state_rnn), f(c), f(c_ctx)
    in_maps = []
    for i in range(NCORES):
        cT = np.zeros((128, 16), np.float32)
        cT[:, 0::2] = colT(c_ctx)
        cT[:, 1::2] = colT(c[i])
        m = dict(shared)
        m.update(
            xp=x_prompt[4 * i:4 * i + 4].reshape(1024, D),
            xs=x_sample[i],
            ckT=np.ascontiguousarray(cache_k[i, 0].transpose(2, 1, 0)),
            cv=cache_v[i, 0].reshape(256, 128),
            stT=colT(state_rnn[i, 0].reshape(-1)),
            cT=cT,
        )
        in_maps.append(m)
    if "nc" not in _NC_CACHE:
        _NC_CACHE["nc"] = build_nc()
    res = run_bass_kernel_spmd(_NC_CACHE["nc"], in_maps, core_ids=list(range(NCORES)))
    R = res.results
    _LAST["R"] = R
    y_prompt = np.concatenate([r["y_p"].reshape(4, SEQP, D) for r in R], axis=0).astype(np.float32)
    y_sample = np.stack([r["y_s"] for r in R], axis=0).astype(np.float32)
    new_k = np.concatenate([r["nk"].reshape(4, 1, SEQP, 2, 64) for r in R], axis=0).astype(np.float32)
    new_v = np.concatenate([r["nv"].reshape(4, 1, SEQP, 2, 64) for r in R], axis=0).astype(np.float32)
    new_rnn = np.concatenate([r["rn"].reshape(4, 1, 2, 512) for r in R], axis=0).astype(np.float32)
    return (y_prompt, y_sample, new_k, new_v, new_rnn)
```
